# Optimizing a Trainium2 kernel written in Bass

```python
import math
import jax, jax.numpy as jnp
from jax import lax
import numpy as np

D_MODEL = 1024
BATCH = 8
SEQ = 4096
DEPTH = 4

CHUNK = 64
QBLOCK = 128
ROPE_THETA = 10000.0
NORM_EPS = 1e-6
N_MIXERS = 3

A_HEADS = 8
A_QK_DIM = 64
A_V_DIM = 2 * A_QK_DIM
A_WIDTH = A_HEADS * A_V_DIM
A_QK_WIDTH = A_HEADS * 2 * A_QK_DIM
A_IN = 2 * A_QK_WIDTH + A_WIDTH + A_WIDTH

B_WIDTH = D_MODEL
B_CONV = 31
B_IN = 3 * B_WIDTH

C_HEADS = 8
C_NOPE = 128
C_ROPE = 64
C_QK = C_NOPE + C_ROPE
C_V = 128
C_Q_LORA = 256
C_KV_LORA = 128
C_WIDTH = C_HEADS * C_V
C_IN = C_Q_LORA + C_KV_LORA + C_ROPE + C_WIDTH

N_A = (DEPTH + 2) // 3
N_B = (DEPTH + 1) // 3
N_C = DEPTH // 3

kernel_name = 'hybrid_diffattn_conformerconv_mla_chunkcausal'


def _rms_norm(x, g):
    xf = x.astype(jnp.float32)
    y = xf * lax.rsqrt(jnp.mean(xf * xf, axis=-1, keepdims=True) + NORM_EPS)
    return (y * g.astype(jnp.float32)).astype(x.dtype)


def _layer_norm(x, g, b):
    xf = x.astype(jnp.float32)
    mu = jnp.mean(xf, axis=-1, keepdims=True)
    xc = xf - mu
    y = xc * lax.rsqrt(jnp.mean(xc * xc, axis=-1, keepdims=True) + NORM_EPS)
    return (y * g.astype(jnp.float32) + b.astype(jnp.float32)).astype(x.dtype)


def _rope_cos_sin(seq, dim):
    inv = 1.0 / (ROPE_THETA ** (jnp.arange(0, dim, 2, dtype=jnp.float32) / dim))
    ang = jnp.arange(seq, dtype=jnp.float32)[:, None] * inv[None, :]
    return jnp.cos(ang), jnp.sin(ang)


def _apply_rope(x, cos, sin):
    xf = x.astype(jnp.float32)
    x1, x2 = jnp.split(xf, 2, axis=-1)
    return jnp.concatenate([x1 * cos - x2 * sin, x2 * cos + x1 * sin], axis=-1).astype(x.dtype)


def _chunk_causal_softmax(s, q_start):
    qb, kl = s.shape[-2], s.shape[-1]
    q_chunk = (q_start + jnp.arange(qb)) // CHUNK
    k_chunk = jnp.arange(kl) // CHUNK
    allowed = k_chunk[None, :] <= q_chunk[:, None]
    return jax.nn.softmax(jnp.where(allowed, s, -jnp.inf), axis=-1)


def _diff_attention(h, w_in, q_g, k_g, lq1, lk1, lq2, lk2, sub_g, w_out, lambda_init):
    bsz, seq, _ = h.shape
    proj = h @ w_in
    q, k, v, gate = jnp.split(proj, [A_QK_WIDTH, 2 * A_QK_WIDTH, 2 * A_QK_WIDTH + A_WIDTH], axis=-1)
    q = q.reshape(bsz, seq, A_HEADS, 2, A_QK_DIM)
    k = k.reshape(bsz, seq, A_HEADS, 2, A_QK_DIM)
    v = v.reshape(bsz, seq, A_HEADS, A_V_DIM).transpose(0, 2, 1, 3).astype(jnp.float32)
    cos, sin = _rope_cos_sin(seq, A_QK_DIM)
    cos, sin = cos[:, None, None, :], sin[:, None, None, :]
    q = _apply_rope(_rms_norm(q, q_g), cos, sin).transpose(0, 2, 3, 1, 4).astype(jnp.float32)
    k = _apply_rope(_rms_norm(k, k_g), cos, sin).transpose(0, 2, 3, 1, 4).astype(jnp.float32)
    f32 = jnp.float32
    lam = (jnp.exp(jnp.sum(lq1.astype(f32) * lk1.astype(f32)))
           - jnp.exp(jnp.sum(lq2.astype(f32) * lk2.astype(f32))) + lambda_init)
    scale = A_QK_DIM ** -0.5
    outs = []
    for s0 in range(0, seq, QBLOCK):
        e = s0 + QBLOCK
        s = jnp.einsum('bhmqd,bhmkd->bhmqk', q[:, :, :, s0:e], k[:, :, :, :e]) * scale
        p = _chunk_causal_softmax(s, s0)
        wts = p[:, :, 0] - lam * p[:, :, 1]
        outs.append(jnp.einsum('bhqk,bhkd->bhqd', wts, v[:, :, :e]))
    o = jnp.concatenate(outs, axis=2)
    o = _rms_norm(o, sub_g) * (1.0 - lambda_init)
    o = o.transpose(0, 2, 1, 3).reshape(bsz, seq, A_WIDTH).astype(h.dtype)
    return (o * jax.nn.silu(gate)) @ w_out


def _conformer_conv(h, w_in, b_in, conv_w, conv_b, ln_g, ln_b, w_out):
    proj = h @ w_in + b_in
    a, b, gate = jnp.split(proj, 3, axis=-1)
    u = a * jax.nn.sigmoid(b)
    u = lax.conv_general_dilated(
        u, conv_w[:, None, :].astype(u.dtype), window_strides=(1,),
        padding=[(B_CONV - 1, 0)], dimension_numbers=('NWC', 'WIO', 'NWC'),
        feature_group_count=B_WIDTH) + conv_b
    u = jax.nn.silu(_layer_norm(u, ln_g, ln_b))
    return (u * jax.nn.silu(gate)) @ w_out


def _mla(h, w_in, cq_g, w_uq, ckv_g, w_ukv, q_g, k_g, w_out):
    bsz, seq, _ = h.shape
    proj = h @ w_in
    c_q, c_kv, k_rope, gate = jnp.split(
        proj, [C_Q_LORA, C_Q_LORA + C_KV_LORA, C_Q_LORA + C_KV_LORA + C_ROPE], axis=-1)
    q = (_rms_norm(c_q, cq_g) @ w_uq).reshape(bsz, seq, C_HEADS, C_QK)
    kv = (_rms_norm(c_kv, ckv_g) @ w_ukv).reshape(bsz, seq, C_HEADS, C_NOPE + C_V)
    q_nope, q_rope = jnp.split(q, [C_NOPE], axis=-1)
    k_nope, v = jnp.split(kv, [C_NOPE], axis=-1)
    cos, sin = _rope_cos_sin(seq, C_ROPE)
    q_nope = _rms_norm(q_nope, q_g[:C_NOPE])
    q_rope = _apply_rope(_rms_norm(q_rope, q_g[C_NOPE:]), cos[:, None, :], sin[:, None, :])
    k_nope = _rms_norm(k_nope, k_g[:C_NOPE])
    k_rope = _apply_rope(_rms_norm(k_rope, k_g[C_NOPE:]), cos, sin).astype(jnp.float32)
    qn = q_nope.transpose(0, 2, 1, 3).astype(jnp.float32)
    qr = q_rope.transpose(0, 2, 1, 3).astype(jnp.float32)
    kn = k_nope.transpose(0, 2, 1, 3).astype(jnp.float32)
    vv = v.transpose(0, 2, 1, 3).astype(jnp.float32)
    scale = C_QK ** -0.5
    outs = []
    for s0 in range(0, seq, QBLOCK):
        e = s0 + QBLOCK
        s = (jnp.einsum('bhqd,bhkd->bhqk', qn[:, :, s0:e], kn[:, :, :e])
             + jnp.einsum('bhqr,bkr->bhqk', qr[:, :, s0:e], k_rope[:, :e])) * scale
        p = _chunk_causal_softmax(s, s0)
        outs.append(jnp.einsum('bhqk,bhkd->bhqd', p, vv[:, :, :e]))
    o = jnp.concatenate(outs, axis=2).transpose(0, 2, 1, 3).reshape(bsz, seq, C_WIDTH).astype(h.dtype)
    return (o * jax.nn.silu(gate)) @ w_out


def setup_inputs(seed: int = 0) -> dict:
    key = jax.random.key(seed)
    keys = iter(jax.random.split(key, 40))

    def nrm(shape, scale):
        return jax.random.normal(next(keys), shape, dtype=jnp.float32) * scale

    def gain(shape):
        return 1.0 + nrm(shape, 0.02)

    out_scale = 0.5
    return {
        'x': nrm((BATCH, SEQ, D_MODEL), 1.0),
        'a_norm_g': gain((N_A, D_MODEL)),
        'a_w_in': nrm((N_A, D_MODEL, A_IN), D_MODEL ** -0.5),
        'a_q_norm_g': gain((N_A, A_QK_DIM)),
        'a_k_norm_g': gain((N_A, A_QK_DIM)),
        'a_lam_q1': nrm((N_A, A_QK_DIM), 0.1),
        'a_lam_k1': nrm((N_A, A_QK_DIM), 0.1),
        'a_lam_q2': nrm((N_A, A_QK_DIM), 0.1),
        'a_lam_k2': nrm((N_A, A_QK_DIM), 0.1),
        'a_sub_norm_g': gain((N_A, A_V_DIM)),
        'a_w_out': nrm((N_A, A_WIDTH, D_MODEL), out_scale * A_WIDTH ** -0.5),
        'b_norm_g': gain((N_B, D_MODEL)),
        'b_w_in': nrm((N_B, D_MODEL, B_IN), D_MODEL ** -0.5),
        'b_b_in': nrm((N_B, B_IN), 0.01),
        'b_conv_w': nrm((N_B, B_CONV, B_WIDTH), B_CONV ** -0.5),
        'b_conv_b': nrm((N_B, B_WIDTH), 0.01),
        'b_ln_g': gain((N_B, B_WIDTH)),
        'b_ln_b': nrm((N_B, B_WIDTH), 0.01),
        'b_w_out': nrm((N_B, B_WIDTH, D_MODEL), out_scale * B_WIDTH ** -0.5),
        'c_norm_g': gain((N_C, D_MODEL)),
        'c_w_in': nrm((N_C, D_MODEL, C_IN), D_MODEL ** -0.5),
        'c_cq_norm_g': gain((N_C, C_Q_LORA)),
        'c_w_uq': nrm((N_C, C_Q_LORA, C_HEADS * C_QK), C_Q_LORA ** -0.5),
        'c_ckv_norm_g': gain((N_C, C_KV_LORA)),
        'c_w_ukv': nrm((N_C, C_KV_LORA, C_HEADS * (C_NOPE + C_V)), C_KV_LORA ** -0.5),
        'c_q_norm_g': gain((N_C, C_QK)),
        'c_k_norm_g': gain((N_C, C_QK)),
        'c_w_out': nrm((N_C, C_WIDTH, D_MODEL), out_scale * C_WIDTH ** -0.5),
    }


def reference(x, a_norm_g, a_w_in, a_q_norm_g, a_k_norm_g, a_lam_q1, a_lam_k1, a_lam_q2,
              a_lam_k2, a_sub_norm_g, a_w_out, b_norm_g, b_w_in, b_b_in, b_conv_w, b_conv_b,
              b_ln_g, b_ln_b, b_w_out, c_norm_g, c_w_in, c_cq_norm_g, c_w_uq, c_ckv_norm_g,
              c_w_ukv, c_q_norm_g, c_k_norm_g, c_w_out):
    for i in range(DEPTH):
        kind, j = i % N_MIXERS, i // N_MIXERS
        if kind == 0:
            lambda_init = 0.8 - 0.6 * math.exp(-0.3 * i)
            h = _rms_norm(x, a_norm_g[j])
            x = x + _diff_attention(h, a_w_in[j], a_q_norm_g[j], a_k_norm_g[j], a_lam_q1[j],
                                    a_lam_k1[j], a_lam_q2[j], a_lam_k2[j], a_sub_norm_g[j],
                                    a_w_out[j], lambda_init)
        elif kind == 1:
            h = _rms_norm(x, b_norm_g[j])
            x = x + _conformer_conv(h, b_w_in[j], b_b_in[j], b_conv_w[j], b_conv_b[j],
                                    b_ln_g[j], b_ln_b[j], b_w_out[j])
        else:
            h = _rms_norm(x, c_norm_g[j])
            x = x + _mla(h, c_w_in[j], c_cq_norm_g[j], c_w_uq[j], c_ckv_norm_g[j], c_w_ukv[j],
                         c_q_norm_g[j], c_k_norm_g[j], c_w_out[j])
    return x
```

```python
import math
from contextlib import ExitStack

import numpy as np
import concourse.bass as bass
import concourse.mybir as mybir
from concourse.bass_utils import run_bass_kernel_spmd

F32 = mybir.dt.float32
BF16 = mybir.dt.bfloat16
AF = mybir.ActivationFunctionType
ALU = mybir.AluOpType
AX = mybir.AxisListType

PE, ACT, DVE, POOL, SP = 0, 1, 2, 3, 4
NDMA = 12

SEQ, DM, NT = 4096, 1024, 32
EPS = 1e-6
NCORES = 8


class Buf:
    def __init__(self, t, name, disjoint=False):
        self.t = t
        self.name = name
        self.writers = {}
        self.readers = {}
        self.disjoint = disjoint

    def __getitem__(self, k):
        return self.t[k]


class Sched:
    def __init__(self, nc, stack):
        self.nc = nc
        self.eng = [nc.tensor, nc.scalar, nc.vector, nc.gpsimd, nc.sync]
        self.nE = 5 + NDMA
        self.sems = [stack.enter_context(nc.semaphore("s%d" % i)) for i in range(self.nE)]
        self.count = [0] * self.nE
        self.clk = [[0] * self.nE for _ in range(self.nE)]
        self.hist = [[None] for _ in range(self.nE)]
        self.dma_rr = 0
        self.nwaits = 0
        self.ninst = 0

    def _semval(self, e, c):
        return c * 16 if e >= 5 else c

    def _deps(self, reads, writes):
        deps = {}
        for t in reads:
            for e, c in t.writers.items():
                if deps.get(e, 0) < c:
                    deps[e] = c
        for t in writes:
            for e, c in t.readers.items():
                if deps.get(e, 0) < c:
                    deps[e] = c
            if not (t.disjoint and not t.readers):
                for e, c in t.writers.items():
                    if deps.get(e, 0) < c:
                        deps[e] = c
        return deps

    def _emit_waits(self, q, deps):
        clk = self.clk[q]
        h = self.eng[q]
        for e2, c in sorted(deps.items()):
            if e2 == q:
                if q == PE:
                    continue
                if c < self.count[q] - 1:
                    continue
            if c <= clk[e2]:
                continue
            h.wait_ge(self.sems[e2], self._semval(e2, c))
            self.nwaits += 1
            hv = self.hist[e2][c]
            for k in range(self.nE):
                if hv[k] > clk[k]:
                    clk[k] = hv[k]
            if clk[e2] < c:
                clk[e2] = c

    def _commit(self, e, ins, reads, writes, snap):
        self.count[e] += 1
        c = self.count[e]
        ins.then_inc(self.sems[e], 16 if e >= 5 else 1)
        self.ninst += 1
        snap = list(snap)
        if e == PE:
            snap[e] = c
        self.hist[e].append(snap)
        for t in reads:
            if t.readers.get(e, 0) < c:
                t.readers[e] = c
        for t in writes:
            if t.readers or not t.disjoint:
                t.writers = {e: c}
                t.readers = {}
            else:
                t.writers[e] = c

    def op(self, e, fn, reads=(), writes=()):
        self._emit_waits(e, self._deps(reads, writes))
        ins = fn(self.eng[e])
        self._commit(e, ins, reads, writes, self.clk[e])
        return ins

    def dma(self, out, in_, reads=(), writes=(), q=SP, **kw):
        j = 5 + self.dma_rr
        self.dma_rr = (self.dma_rr + 1) % NDMA
        deps = self._deps(reads, writes)
        if self.count[j] > 0:
            deps[j] = max(deps.get(j, 0), self.count[j])
        self._emit_waits(q, deps)
        ins = self.eng[q].dma_start(out=out, in_=in_, **kw)
        self._commit(j, ins, reads, writes, self.clk[q])
        return ins

    def barrier(self):
        tot = {e: self.count[e] for e in range(self.nE) if self.count[e] > 0}
        for q in range(5):
            clk = self.clk[q]
            h = self.eng[q]
            for e2, c in sorted(tot.items()):
                if e2 == q and q == PE:
                    continue
                if c <= clk[e2]:
                    continue
                h.wait_ge(self.sems[e2], self._semval(e2, c))
                self.nwaits += 1
                clk[e2] = c


class K:
    def __init__(self, nc, st):
        self.nc = nc
        self.st = st
        self.S = Sched(nc, st)
        self.n = 0

    def sb(self, stack, shape, dt, name=None):
        self.n += 1
        nm = "%s_%d" % (name or "t", self.n)
        return Buf(stack.enter_context(self.nc.sbuf_tensor(nm, list(shape), dt)), nm)

    def ps(self, stack, shape, dt=F32, name=None):
        self.n += 1
        nm = "%s_%d" % (name or "p", self.n)
        return Buf(stack.enter_context(self.nc.psum_tensor(nm, list(shape), dt)), nm)


def bc(ap, shape):
    return ap.unsqueeze(1).broadcast_to(list(shape))


def build_program(nlayers=4, dbg=None, kinds=None):
    nc = bass.Bass("TRN2", target_bir_lowering=False)
    names = {}

    def din(name, shape):
        names[name] = nc.dram_tensor(name, list(shape), F32, kind="ExternalInput").ap()
        return names[name]

    x_in = din("x", [SEQ, DM])
    ident_d = din("ident", [128, 128])
    cs4_d = din("cs4", [SEQ, 128])
    A = dict(norm_g=din("a_norm_g", [2, DM]), w_in=din("a_w_in", [2, DM, 4096]), q_g=din("a_q_norm_g", [2, 64]),
             k_g=din("a_k_norm_g", [2, 64]), lq1=din("a_lam_q1", [2, 64]), lk1=din("a_lam_k1", [2, 64]),
             lq2=din("a_lam_q2", [2, 64]), lk2=din("a_lam_k2", [2, 64]), sub_g=din("a_sub_norm_g", [2, 128]),
             w_out=din("a_w_out", [2, DM, DM]))
    Bp = dict(norm_g=din("b_norm_g", [1, DM]), w_in=din("b_w_in", [1, DM, 3072]), b_in=din("b_b_in", [1, 3072]),
              conv_w=din("b_conv_w", [1, 31, DM]), conv_b=din("b_conv_b", [1, DM]), ln_g=din("b_ln_g", [1, DM]),
              ln_b=din("b_ln_b", [1, DM]), w_out=din("b_w_out", [1, DM, DM]))
    C = dict(norm_g=din("c_norm_g", [1, DM]), w_in=din("c_w_in", [1, DM, 1472]), cq_g=din("c_cq_norm_g", [1, 256]),
             w_uq=din("c_w_uq", [1, 256, 1536]), ckv_g=din("c_ckv_norm_g", [1, 128]), w_ukv=din("c_w_ukv", [1, 128, 2048]),
             q_g=din("c_q_norm_g", [1, 192]), k_g=din("c_k_norm_g", [1, 192]), w_out=din("c_w_out", [1, DM, DM]))
    out_d = nc.dram_tensor("out", [SEQ, DM], F32, kind="ExternalOutput").ap()

    def scratch(name, shape, dt):
        return Buf(nc.dram_tensor(name, list(shape), dt, kind="Internal").ap(), name, disjoint=True)

    XS = [scratch("xs0", [SEQ, DM], F32), scratch("xs1", [SEQ, DM], F32)]
    QT = scratch("qt", [8, 128, SEQ], BF16)
    KT = scratch("kt", [8, 128, SEQ], BF16)
    QR = scratch("qr", [8, 64, SEQ], BF16)
    KR = scratch("kr", [64, SEQ], BF16)
    VS = scratch("vs", [SEQ, 8, 130], BF16)
    GS = scratch("gs", [SEQ, DM], BF16)
    OGT = scratch("ogt", [8, 128, SEQ], BF16)
    OUT = Buf(out_d, "out", disjoint=True)
    XIN = Buf(x_in, "xin", disjoint=True)
    dbg_out = None
    if dbg is not None:
        dbg_out = {}
        for nm, shp, dt in dbg:
            dbg_out[nm] = nc.dram_tensor("dbg_" + nm, list(shp), dt, kind="ExternalOutput").ap()

    with ExitStack() as st:
        k = K(nc, st)
        S = k.S
        ident = k.sb(st, [128, 128], F32, "ident")
        S.dma(ident[:], ident_d, writes=[ident])

        def colvec(ph, psb, src2d, n):
            dst = k.sb(ph, [128, n], F32, "cv")
            S.dma(dst[:], src2d.rearrange("c p -> p c"), writes=[dst], allow_slow_non_contiguous=True)
            return dst

        def load_w(ph, dst, src, nch, F, gcol=None, stage=None, gfn=None):
            if gcol is None:
                S.dma(dst[:], src.rearrange("(c p) f -> p c f", p=128), writes=[dst], q=POOL)
                return
            FH = min(F, 2048)
            i = 0
            for c in range(nch):
                for f0 in range(0, F, FH):
                    sg = stage[i % 2]
                    i += 1
                    fw = min(FH, F - f0)
                    S.dma(sg[:, 0:fw], src[c * 128:(c + 1) * 128, f0:f0 + fw], writes=[sg])
                    sc = gfn(c)
                    S.op(POOL, lambda e: e.tensor_scalar(out=dst[:, c, f0:f0 + fw], in0=sg[:, 0:fw], scalar1=sc, scalar2=None,
                                                          op0=ALU.mult), [sg, gcol], [dst])

        def front(t, xsrc, xt, stt_, xT, psT, ng=8):
            S.dma(xt[:], xsrc[t * 128:(t + 1) * 128, :], reads=[xsrc], writes=[xt])
            junk = front.junk
            S.op(ACT, lambda e: e.activation(out=junk[:], in_=xt[:], func=AF.Square, accum_out=stt_[:, 0:1]), [xt], [junk, stt_])
            S.op(ACT, lambda e: e.activation(out=stt_[:, 1:2], in_=stt_[:, 0:1], func=AF.Ln, scale=1.0 / DM, bias=EPS), [stt_], [stt_])
            S.op(ACT, lambda e: e.activation(out=stt_[:, 2:3], in_=stt_[:, 1:2], func=AF.Exp, scale=-0.5), [stt_], [stt_])
            S.op(DVE, lambda e: e.tensor_scalar(out=stt_[:, 3:4], in0=stt_[:, 2:3], scalar1=-1.0, scalar2=None, op0=ALU.mult), [stt_], [stt_])
            for hb in range(2):
                for i in range(4):
                    c = hb * 4 + i
                    S.op(PE, lambda e: e.transpose(out=psT[hb][:, i * 128:(i + 1) * 128], in_=xt[:, c * 128:(c + 1) * 128], identity=ident[:]),
                         [xt, ident], [psT[hb]])
                S.op(DVE, lambda e: e.tensor_copy(out=xT[:, hb * 4:(hb + 1) * 4, :], in_=psT[hb][:].rearrange("p (a b) -> p a b", a=4)),
                     [psT[hb]], [xT])
            return stt_[:, 2:3], stt_[:, 3:4]

        def project(xT, W, f0, fw, bank, nch=8):
            for c in range(nch):
                S.op(PE, lambda e: e.matmul(bank[:, 0:fw], lhsT=xT[:, c, :], rhs=W[:, c, f0:f0 + fw], start=(c == 0), stop=(c == nch - 1)),
                     [xT, W], [bank])

        def silu_gate(bank, rstd, nrstd, e_, out_ap, outbuf, bias_row=None, gf=None, width=512):
            if bias_row is None:
                S.op(ACT, lambda e: e.activation(out=e_[:, 0:width], in_=bank[:, 0:width], func=AF.Exp, scale=nrstd), [bank, front.stt_cur], [e_])
            else:
                S.op(DVE, lambda e: e.scalar_tensor_tensor(out=gf[:, 0:width], in0=bank[:, 0:width], scalar=rstd, in1=bias_row, op0=ALU.mult, op1=ALU.add),
                     [bank, front.stt_cur, front.brow], [gf])
                S.op(ACT, lambda e: e.activation(out=e_[:, 0:width], in_=gf[:, 0:width], func=AF.Exp, scale=-1.0), [gf], [e_])
            S.op(DVE, lambda e: e.tensor_scalar(out=e_[:, 0:width], in0=e_[:, 0:width], scalar1=1.0, scalar2=None, op0=ALU.add), [e_], [e_])
            S.op(DVE, lambda e: e.reciprocal(out=e_[:, 0:width], in_=e_[:, 0:width]), [e_], [e_])
            if bias_row is None:
                S.op(DVE, lambda e: e.scalar_tensor_tensor(out=out_ap, in0=bank[:, 0:width], scalar=rstd, in1=e_[:, 0:width], op0=ALU.mult, op1=ALU.mult),
                     [bank, front.stt_cur, e_], [outbuf])
            else:
                S.op(DVE, lambda e: e.tensor_tensor(out=out_ap, in0=gf[:, 0:width], in1=e_[:, 0:width], op=ALU.mult), [gf, e_], [outbuf])

        def group_rs(src, ng, gs, sq, ssg, rs, off=0):
            n = ng * gs
            S.op(DVE, lambda e: e.tensor_tensor(out=sq[:, 0:n], in0=src, in1=src, op=ALU.mult), [group_rs.srcbuf], [sq])
            S.op(DVE, lambda e: e.tensor_reduce(out=ssg[:, off:off + ng], in_=sq[:, 0:n].rearrange("p (g d) -> p g d", g=ng), axis=AX.X, op=ALU.add),
                 [sq], [ssg])
            S.op(ACT, lambda e: e.activation(out=ssg[:, off:off + ng], in_=ssg[:, off:off + ng], func=AF.Ln, scale=1.0 / gs, bias=EPS), [ssg], [ssg])
            S.op(ACT, lambda e: e.activation(out=rs[:, off:off + ng], in_=ssg[:, off:off + ng], func=AF.Exp, scale=-0.5), [ssg], [rs])

        def rope(n3, T, ro3, ta, tb, ng, nbuf, Tbuf, robuf):
            n1, n2 = n3[:, :, 0:32], n3[:, :, 32:64]
            C1, S2, C2, S1 = [bc(T[:, i * 32:(i + 1) * 32], [128, ng, 32]) for i in range(4)]
            a3 = ta[:, 0:ng * 32].rearrange("p (g d) -> p g d", g=ng)
            b3 = tb[:, 0:ng * 32].rearrange("p (g d) -> p g d", g=ng)
            S.op(DVE, lambda e: e.tensor_tensor(out=a3, in0=n1, in1=C1, op=ALU.mult), [nbuf, Tbuf], [ta])
            S.op(DVE, lambda e: e.tensor_tensor(out=b3, in0=n2, in1=S2, op=ALU.mult), [nbuf, Tbuf], [tb])
            S.op(DVE, lambda e: e.tensor_tensor(out=ro3[:, :, 0:32], in0=a3, in1=b3, op=ALU.subtract), [ta, tb], [robuf])
            S.op(DVE, lambda e: e.tensor_tensor(out=a3, in0=n2, in1=C2, op=ALU.mult), [nbuf, Tbuf], [ta])
            S.op(DVE, lambda e: e.tensor_tensor(out=b3, in0=n1, in1=S1, op=ALU.mult), [nbuf, Tbuf], [tb])
            S.op(DVE, lambda e: e.tensor_tensor(out=ro3[:, :, 32:64], in0=a3, in1=b3, op=ALU.add), [ta, tb], [robuf])

        def transp_store(src, nblk, bw, psQ, stg, stg_off, srcbuf, evac=ACT):
            i = 0
            qi = transp_store.qi
            while i < nblk:
                nb = min(4, nblk - i)
                bank = psQ[qi % 2]
                qi += 1
                for b in range(nb):
                    S.op(PE, lambda e: e.transpose(out=bank[0:bw, b * 128:(b + 1) * 128], in_=src[:, (i + b) * bw:(i + b + 1) * bw], identity=ident[:]),
                         [srcbuf, ident], [bank])
                if evac == ACT:
                    S.op(ACT, lambda e: e.activation(out=stg[0:bw, stg_off + i:stg_off + i + nb, :], in_=bank[0:bw, 0:nb * 128].rearrange("p (a b) -> p a b", a=nb),
                                                      func=AF.Copy), [bank], [stg])
                else:
                    S.op(DVE, lambda e: e.tensor_copy(out=stg[0:bw, stg_off + i:stg_off + i + nb, :], in_=bank[0:bw, 0:nb * 128].rearrange("p (a b) -> p a b", a=nb)),
                         [bank], [stg])
                i += nb
            transp_store.qi = qi
        transp_store.qi = 0

        def phase_T1_attn(kind, j, xsrc, lambda_init=None):
            with ExitStack() as ph:
                FW = 4096 if kind == "A" else 1472
                P = A if kind == "A" else C
                W = k.sb(ph, [128, 8, FW], BF16, "W")
                stage = [k.sb(ph, [128, 2048], F32, "stg") for _ in range(2)]
                psT = [k.ps(ph, [128, 512]) for _ in range(2)]
                psP = [k.ps(ph, [128, 512]) for _ in range(4)]
                psQ = [k.ps(ph, [128, 512]) for _ in range(2)]
                gcol = colvec(ph, psT[0], P["norm_g"][j:j + 1, :].rearrange("o (c p) -> (o c) p", p=128), 8)
                load_w(ph, W, P["w_in"][j], 8, FW, gcol=gcol, stage=stage, gfn=lambda c: gcol[:, c:c + 1])
                cs4 = k.sb(ph, [128, NT, 128], F32, "cs4")
                S.dma(cs4[:], cs4_d.rearrange("(t p) d -> p t d", p=128), writes=[cs4])
                gq = k.sb(ph, [128, 64], F32, "gq")
                gk = k.sb(ph, [128, 64], F32, "gk")
                G4q = k.sb(ph, [128, 128], F32, "G4q")
                G4k = k.sb(ph, [128, 128], F32, "G4k")
                if kind == "A":
                    S.dma(gq[:], P["q_g"][j:j + 1, :].partition_broadcast(128), writes=[gq])
                    S.dma(gk[:], P["k_g"][j:j + 1, :].partition_broadcast(128), writes=[gk])
                    qscale = 64 ** -0.5
                else:
                    S.dma(gq[:], P["q_g"][j:j + 1, 128:192].partition_broadcast(128), writes=[gq])
                    S.dma(gk[:], P["k_g"][j:j + 1, 128:192].partition_broadcast(128), writes=[gk])
                    qscale = 192 ** -0.5
                for (g_, G4, sc) in ((gq, G4q, qscale), (gk, G4k, 1.0)):
                    for i, (a0, a1) in enumerate(((0, 32), (32, 64), (32, 64), (0, 32))):
                        S.op(DVE, lambda e: e.tensor_scalar(out=G4[:, i * 32:(i + 1) * 32], in0=g_[:, a0:a1], scalar1=sc, scalar2=None, op0=ALU.mult), [g_], [G4])
                xt = [k.sb(ph, [128, DM], F32, "xt") for _ in range(2)]
                stt_ = [k.sb(ph, [128, 4], F32, "stt") for _ in range(2)]
                xT = [k.sb(ph, [128, 8, 128], BF16, "xT") for _ in range(2)]
                front.junk = k.sb(ph, [128, DM], F32, "junk")
                e_ = [k.sb(ph, [128, 512], F32, "e") for _ in range(2)]
                Gst = [k.sb(ph, [128, DM], BF16, "Gst") for _ in range(2)]
                Vst = [k.sb(ph, [128, 8, 130], BF16, "Vst") for _ in range(2)]
                for v in Vst:
                    S.op(DVE, lambda e: e.memset(v[:, :, 128:130], 1.0), [], [v])
                TQ = k.sb(ph, [128, 128], F32, "TQ")
                TK = k.sb(ph, [128, 128], F32, "TK")
                ta = k.sb(ph, [128, 512], F32, "ta")
                tb = k.sb(ph, [128, 512], F32, "tb")
                if kind == "A":
                    qf = k.sb(ph, [128, 2048], F32, "qf")
                    sq = k.sb(ph, [128, 2048], F32, "sq")
                    ssg = k.sb(ph, [128, 32], F32, "ssg")
                    rs = k.sb(ph, [128, 32], F32, "rs")
                    ro = k.sb(ph, [128, 2048], F32, "ro")
                    qTst = [k.sb(ph, [128, 16, 128], BF16, "qTst") for _ in range(2)]
                else:
                    Wuq = k.sb(ph, [128, 2, 1536], BF16, "Wuq")
                    Wukv = k.sb(ph, [128, 1, 2048], BF16, "Wukv")
                    cqg = colvec(ph, psT[0], P["cq_g"][j:j + 1, :].rearrange("o (c p) -> (o c) p", p=128), 2)
                    ckvg = colvec(ph, psT[1], P["ckv_g"][j:j + 1, :].rearrange("o (c p) -> (o c) p", p=128), 1)
                    load_w(ph, Wuq, P["w_uq"][j], 2, 1536, gcol=cqg, stage=stage, gfn=lambda c: cqg[:, c:c + 1])
                    load_w(ph, Wukv, P["w_ukv"][j], 1, 2048, gcol=ckvg, stage=stage, gfn=lambda c: ckvg[:, c:c + 1])
                    gqn = k.sb(ph, [128, 128], F32, "gqn")
                    gkn = k.sb(ph, [128, 128], F32, "gkn")
                    S.dma(gqn[:], P["q_g"][j:j + 1, 0:128].partition_broadcast(128), writes=[gqn])
                    S.dma(gkn[:], P["k_g"][j:j + 1, 0:128].partition_broadcast(128), writes=[gkn])
                    S.op(DVE, lambda e: e.tensor_scalar(out=gqn[:], in0=gqn[:], scalar1=qscale, scalar2=None, op0=ALU.mult), [gqn], [gqn])
                    lat = k.sb(ph, [128, 448], F32, "lat")
                    latT = [k.sb(ph, [128, 3, 128], BF16, "latT") for _ in range(2)]
                    qf = k.sb(ph, [128, 1536], F32, "qf")
                    kvf = k.sb(ph, [128, 2048], F32, "kvf")
                    sq = k.sb(ph, [128, 1536], F32, "sq")
                    ssg = k.sb(ph, [128, 32], F32, "ssg")
                    rs = k.sb(ph, [128, 32], F32, "rs")
                    qn = k.sb(ph, [128, 1024], F32, "qn")
                    kn = k.sb(ph, [128, 1024], F32, "kn")
                    qr = k.sb(ph, [128, 512], F32, "qr")
                    qro = k.sb(ph, [128, 512], F32, "qro")
                    krn = k.sb(ph, [128, 64], F32, "krn")
                    kro = k.sb(ph, [128, 64], F32, "kro")
                    qTst = [k.sb(ph, [128, 8, 128], BF16, "qTst") for _ in range(2)]
                    kTst = [k.sb(ph, [128, 8, 128], BF16, "kTst") for _ in range(2)]
                    qRst = [k.sb(ph, [64, 8, 128], BF16, "qRst") for _ in range(2)]
                    kRst = [k.sb(ph, [64, 1, 128], BF16, "kRst") for _ in range(2)]
                    lst = k.sb(ph, [128, 4], F32, "lst")
                pi = 0
                for t in range(NT):
                    b2 = t % 2
                    front.stt_cur = stt_[b2]
                    rstd, nrstd = front(t, xsrc, xt[b2], stt_[b2], xT[b2], psT)
                    sc = stt_[b2]
                    tsl = slice(t * 128, (t + 1) * 128)
                    S.op(DVE, lambda e: e.tensor_tensor(out=TQ[:], in0=cs4[:, t, :], in1=G4q[:], op=ALU.mult), [cs4, G4q], [TQ])
                    S.op(DVE, lambda e: e.tensor_tensor(out=TK[:], in0=cs4[:, t, :], in1=G4k[:], op=ALU.mult), [cs4, G4k], [TK])
                    if kind == "A":
                        for fg in range(8):
                            bank = psP[pi % 4]
                            pi += 1
                            project(xT[b2], W, fg * 512, 512, bank)
                            if fg < 4:
                                S.op(ACT, lambda e: e.activation(out=qf[:, fg * 512:(fg + 1) * 512], in_=bank[:], func=AF.Identity, scale=rstd), [bank, sc], [qf])
                            elif fg < 6:
                                S.op(ACT, lambda e: e.activation(out=Vst[b2][:, (fg - 4) * 4:(fg - 4) * 4 + 4, 0:128], in_=bank[:].rearrange("p (a b) -> p a b", a=4),
                                                                  func=AF.Identity, scale=rstd), [bank, sc], [Vst[b2]])
                            else:
                                h0 = (fg - 6) * 512
                                silu_gate(bank, rstd, nrstd, e_[fg % 2], Gst[b2][:, h0:h0 + 512], Gst[b2])
                        group_rs.srcbuf = qf
                        group_rs(qf[:, 0:2048], 32, 64, sq, ssg, rs)
                        S.op(DVE, lambda e: e.tensor_tensor(out=sq[:].rearrange("p (g d) -> p g d", g=32), in0=qf[:].rearrange("p (g d) -> p g d", g=32),
                                                            in1=rs[:, 0:32].unsqueeze(2).broadcast_to([128, 32, 64]), op=ALU.mult), [qf, rs], [sq])
                        for half, T in ((0, TQ), (1, TK)):
                            for q4 in range(2):
                                o0 = half * 1024 + q4 * 512
                                rope(sq[:, o0:o0 + 512].rearrange("p (g d) -> p g d", g=8), T, ro[:, o0:o0 + 512].rearrange("p (g d) -> p g d", g=8),
                                     ta, tb, 8, sq, T, ro)
                        transp_store(ro, 16, 128, psQ, qTst[b2], 0, ro)
                        S.dma(QT[:, :, tsl].rearrange("h p t -> p h t"), qTst[b2][:, 0:8, :], reads=[qTst[b2]], writes=[QT])
                        S.dma(KT[:, :, tsl].rearrange("h p t -> p h t"), qTst[b2][:, 8:16, :], reads=[qTst[b2]], writes=[KT])
                    else:
                        bank = psP[pi % 4]
                        pi += 1
                        project(xT[b2], W, 0, 448, bank)
                        S.op(ACT, lambda e: e.activation(out=lat[:], in_=bank[:, 0:448], func=AF.Identity, scale=rstd), [bank, sc], [lat])
                        for fg in range(2):
                            bank = psP[pi % 4]
                            pi += 1
                            project(xT[b2], W, 448 + fg * 512, 512, bank)
                            silu_gate(bank, rstd, nrstd, e_[fg % 2], Gst[b2][:, fg * 512:(fg + 1) * 512], Gst[b2])
                        group_rs.srcbuf = lat
                        group_rs(lat[:, 0:256], 1, 256, sq, ssg, lst, off=0)
                        group_rs(lat[:, 256:384], 1, 128, sq, ssg, lst, off=1)
                        group_rs(lat[:, 384:448], 1, 64, sq, ssg, lst, off=2)
                        transp_store(lat, 3, 128, psQ, latT[b2], 0, lat, evac=DVE)
                        for fg in range(3):
                            bank = psP[pi % 4]
                            pi += 1
                            project(latT[b2], Wuq, fg * 512, 512, bank, nch=2)
                            S.op(ACT, lambda e: e.activation(out=qf[:, fg * 512:(fg + 1) * 512], in_=bank[:], func=AF.Identity, scale=lst[:, 0:1]), [bank, lst], [qf])
                        for fg in range(4):
                            bank = psP[pi % 4]
                            pi += 1
                            S.op(PE, lambda e: e.matmul(bank[:], lhsT=latT[b2][:, 2, :], rhs=Wukv[:, 0, fg * 512:(fg + 1) * 512], start=True, stop=True),
                                 [latT[b2], Wukv], [bank])
                            S.op(ACT, lambda e: e.activation(out=kvf[:, fg * 512:(fg + 1) * 512], in_=bank[:], func=AF.Identity, scale=lst[:, 1:2]), [bank, lst], [kvf])
                        q3 = qf[:].rearrange("p (h d) -> p h d", h=8)
                        kv3 = kvf[:].rearrange("p (h d) -> p h d", h=8)
                        sq3 = sq[:].rearrange("p (h d) -> p h d", h=8)
                        S.op(DVE, lambda e: e.tensor_tensor(out=sq[:], in0=qf[:], in1=qf[:], op=ALU.mult), [qf], [sq])
                        S.op(DVE, lambda e: e.tensor_reduce(out=ssg[:, 0:8], in_=sq3[:, :, 0:128], axis=AX.X, op=ALU.add), [sq], [ssg])
                        S.op(DVE, lambda e: e.tensor_reduce(out=ssg[:, 8:16], in_=sq3[:, :, 128:192], axis=AX.X, op=ALU.add), [sq], [ssg])
                        S.op(DVE, lambda e: e.tensor_tensor(out=sq[:, 0:1024].rearrange("p (h d) -> p h d", h=8), in0=kv3[:, :, 0:128], in1=kv3[:, :, 0:128], op=ALU.mult),
                             [kvf], [sq])
                        S.op(DVE, lambda e: e.tensor_reduce(out=ssg[:, 16:24], in_=sq[:, 0:1024].rearrange("p (h d) -> p h d", h=8), axis=AX.X, op=ALU.add), [sq], [ssg])
                        S.op(ACT, lambda e: e.activation(out=ssg[:, 0:8], in_=ssg[:, 0:8], func=AF.Ln, scale=1.0 / 128, bias=EPS), [ssg], [ssg])
                        S.op(ACT, lambda e: e.activation(out=ssg[:, 8:16], in_=ssg[:, 8:16], func=AF.Ln, scale=1.0 / 64, bias=EPS), [ssg], [ssg])
                        S.op(ACT, lambda e: e.activation(out=ssg[:, 16:24], in_=ssg[:, 16:24], func=AF.Ln, scale=1.0 / 128, bias=EPS), [ssg], [ssg])
                        S.op(ACT, lambda e: e.activation(out=rs[:, 0:24], in_=ssg[:, 0:24], func=AF.Exp, scale=-0.5), [ssg], [rs])
                        qn3 = qn[:].rearrange("p (h d) -> p h d", h=8)
                        kn3 = kn[:].rearrange("p (h d) -> p h d", h=8)
                        S.op(DVE, lambda e: e.tensor_tensor(out=qn3, in0=q3[:, :, 0:128], in1=rs[:, 0:8].unsqueeze(2).broadcast_to([128, 8, 128]), op=ALU.mult), [qf, rs], [qn])
                        S.op(DVE, lambda e: e.tensor_tensor(out=qn3, in0=qn3, in1=bc(gqn[:], [128, 8, 128]), op=ALU.mult), [qn, gqn], [qn])
                        S.op(DVE, lambda e: e.tensor_tensor(out=kn3, in0=kv3[:, :, 0:128], in1=rs[:, 16:24].unsqueeze(2).broadcast_to([128, 8, 128]), op=ALU.mult), [kvf, rs], [kn])
                        S.op(DVE, lambda e: e.tensor_tensor(out=kn3, in0=kn3, in1=bc(gkn[:], [128, 8, 128]), op=ALU.mult), [kn, gkn], [kn])
                        qr3 = qr[:].rearrange("p (h d) -> p h d", h=8)
                        S.op(DVE, lambda e: e.tensor_tensor(out=qr3, in0=q3[:, :, 128:192], in1=rs[:, 8:16].unsqueeze(2).broadcast_to([128, 8, 64]), op=ALU.mult), [qf, rs], [qr])
                        rope(qr3, TQ, qro[:].rearrange("p (h d) -> p h d", h=8), ta, tb, 8, qr, TQ, qro)
                        S.op(DVE, lambda e: e.tensor_scalar(out=krn[:], in0=lat[:, 384:448], scalar1=lst[:, 2:3], scalar2=None, op0=ALU.mult), [lat, lst], [krn])
                        rope(krn[:].rearrange("p (g d) -> p g d", g=1), TK, kro[:].rearrange("p (g d) -> p g d", g=1), ta, tb, 1, krn, TK, kro)
                        S.op(ACT, lambda e: e.activation(out=Vst[b2][:, :, 0:128], in_=kv3[:, :, 128:256], func=AF.Copy), [kvf], [Vst[b2]])
                        transp_store(qn, 8, 128, psQ, qTst[b2], 0, qn)
                        transp_store(kn, 8, 128, psQ, kTst[b2], 0, kn)
                        transp_store(qro, 8, 64, psQ, qRst[b2], 0, qro)
                        transp_store(kro, 1, 64, psQ, kRst[b2], 0, kro)
                        S.dma(QT[:, :, tsl].rearrange("h p t -> p h t"), qTst[b2][:], reads=[qTst[b2]], writes=[QT])
                        S.dma(KT[:, :, tsl].rearrange("h p t -> p h t"), kTst[b2][:], reads=[kTst[b2]], writes=[KT])
                        S.dma(QR[:, :, tsl].rearrange("h p t -> p h t"), qRst[b2][:], reads=[qRst[b2]], writes=[QR])
                        S.dma(KR[:, tsl], kRst[b2][:, 0, :], reads=[kRst[b2]], writes=[KR])
                    S.dma(VS[tsl, :, :], Vst[b2][:], reads=[Vst[b2]], writes=[VS])
                    S.dma(GS[tsl, :], Gst[b2][:], reads=[Gst[b2]], writes=[GS])
                S.barrier()

        def phase_T2_attn(kind, j, lambda_init=None):
            with ExitStack() as ph:
                P = A if kind == "A" else C
                nmap = 2 if kind == "A" else 1
                psS = k.ps(ph, [128, 2, 2, 512], name="psS")
                pacc = k.ps(ph, [128, 3, 512], name="pacc")
                psO = k.ps(ph, [128, 512], name="psO")
                psSb = [Buf(psS.t, "psS0"), Buf(psS.t, "psS1")]
                QTh = [k.sb(ph, [128, SEQ], BF16, "QTh") for _ in range(2)]
                KTh = [k.sb(ph, [128, SEQ], BF16, "KTh") for _ in range(2)]
                Vh = [k.sb(ph, [128, NT, 130], BF16, "Vh") for _ in range(2)]
                if kind == "C":
                    QRh = [k.sb(ph, [64, SEQ], BF16, "QRh") for _ in range(2)]
                    KRs = k.sb(ph, [64, SEQ], BF16, "KRs")
                    S.dma(KRs[:], KR[:, :], reads=[KR], writes=[KRs])
                PT = [k.sb(ph, [128, 2, 512], BF16, "PT") for _ in range(3)]
                Gt = [k.sb(ph, [128, 4, 128], BF16, "Gt") for _ in range(2)]
                accsb = k.sb(ph, [128, 3, 512], F32, "accsb")
                rz = k.sb(ph, [128, 8], F32, "rz")
                tmp = k.sb(ph, [128, 128], F32, "tmp")
                o4 = k.sb(ph, [128, 4, 128], F32, "o4")
                ogf = k.sb(ph, [128, 4, 128], F32, "ogf")
                junk = k.sb(ph, [128, 128], F32, "junk")
                ssj = k.sb(ph, [128, 8], F32, "ssj")
                ogst = [k.sb(ph, [128, 512], BF16, "ogst") for _ in range(2)]
                neglam = None
                if kind == "A":
                    lv = k.sb(ph, [128, 4, 64], F32, "lv")
                    for i, nm in enumerate(("lq1", "lk1", "lq2", "lk2")):
                        S.dma(lv[:, i, :], P[nm][j:j + 1, :].partition_broadcast(128), writes=[lv])
                    lam = k.sb(ph, [128, 4], F32, "lam")
                    S.op(DVE, lambda e: e.scalar_tensor_tensor(out=junk[:, 0:64], in0=lv[:, 0, :], scalar=1.0, in1=lv[:, 1, :], op0=ALU.mult, op1=ALU.mult,
                                                               accum_out=lam[:, 0:1]), [lv], [junk, lam])
                    S.op(DVE, lambda e: e.scalar_tensor_tensor(out=junk[:, 64:128], in0=lv[:, 2, :], scalar=1.0, in1=lv[:, 3, :], op0=ALU.mult, op1=ALU.mult,
                                                               accum_out=lam[:, 1:2]), [lv], [junk, lam])
                    S.op(ACT, lambda e: e.activation(out=lam[:, 0:2], in_=lam[:, 0:2], func=AF.Exp), [lam], [lam])
                    S.op(DVE, lambda e: e.scalar_tensor_tensor(out=lam[:, 2:3], in0=lam[:, 1:2], scalar=-float(lambda_init), in1=lam[:, 0:1], op0=ALU.add, op1=ALU.subtract),
                         [lam], [lam])
                    neglam = lam
                offs = lambda i: (i // 3) * 512 + (i % 3) * 130
                accf = pacc.t[:].rearrange("p a b -> p (a b)")
                accsf = accsb.t[:].rearrange("p a b -> p (a b)")

                def load_head(h):
                    b = h % 2
                    S.dma(QTh[b][:], QT[h, :, :], reads=[QT], writes=[QTh[b]])
                    S.dma(KTh[b][:], KT[h, :, :], reads=[KT], writes=[KTh[b]])
                    S.dma(Vh[b][:], VS[:, h, :].rearrange("(t p) d -> p t d", p=128), reads=[VS], writes=[Vh[b]])
                    if kind == "C":
                        S.dma(QRh[b][:], QR[h, :, :], reads=[QR], writes=[QRh[b]])

                jobs = []
                for h in range(8):
                    for qb in range(8):
                        nk = 4 * qb + 4
                        for kt in range(nk):
                            jobs.append((h, qb, kt, kt == 0, kt == nk - 1))

                def stage_S(i, job):
                    h, qb, kt, first, last = job
                    hb = h % 2
                    jj = kt - 4 * qb
                    q0 = qb * 512 + (max(jj, 0)) * 128
                    n = (qb + 1) * 512 - q0
                    sb_ = psSb[i % 2]
                    pt = PT[i % 3]
                    ks = slice(kt * 128, (kt + 1) * 128)
                    if kind == "A":
                        for m in range(2):
                            rows = slice(m * 64, (m + 1) * 64)
                            S.op(PE, lambda e: e.matmul(psS.t[:, i % 2, m, 0:n], lhsT=KTh[hb][rows, ks], rhs=QTh[hb][rows, q0:q0 + n], start=True, stop=True),
                                 [KTh[hb], QTh[hb]], [sb_])
                        S.op(ACT, lambda e: e.activation(out=pt[:, :, 0:n], in_=psS.t[:, i % 2, :, 0:n], func=AF.Exp), [sb_], [pt])
                        if jj >= 0:
                            S.op(DVE, lambda e: e.memset(pt[64:128, :, 0:64], 0.0), [], [pt])
                    else:
                        S.op(PE, lambda e: e.matmul(psS.t[:, i % 2, 0, 0:n], lhsT=KTh[hb][:, ks], rhs=QTh[hb][:, q0:q0 + n], start=True, stop=False),
                             [KTh[hb], QTh[hb]], [sb_])
                        S.op(PE, lambda e: e.matmul(psS.t[:, i % 2, 0, 0:n], lhsT=KRs[:, ks], rhs=QRh[hb][:, q0:q0 + n], start=False, stop=True),
                             [KRs, QRh[hb]], [sb_])
                        S.op(ACT, lambda e: e.activation(out=pt[:, 0, 0:n], in_=psS.t[:, i % 2, 0, 0:n], func=AF.Exp), [sb_], [pt])
                        if jj >= 0:
                            S.op(DVE, lambda e: e.memset(pt[64:128, 0, 0:64], 0.0), [], [pt])

                def stage_PV(i, job):
                    h, qb, kt, first, last = job
                    hb = h % 2
                    jj = max(kt - 4 * qb, 0)
                    pt = PT[i % 3]
                    for m in range(nmap):
                        for jq in range(jj, 4):
                            ai = m * 4 + jq
                            o = offs(ai)
                            st_ = first and (ai % 3 == 0)
                            S.op(PE, lambda e: e.matmul(accf[:, o:o + 130], lhsT=pt[:, m, (jq - jj) * 128:(jq - jj + 1) * 128], rhs=Vh[hb][:, kt, :],
                                                        start=st_, stop=last, skip_group_check=True), [pt, Vh[hb]], [pacc])
                    if last:
                        post(h, qb)

                def post(h, qb):
                    gb = Gt[(h * 8 + qb) % 2]
                    ob = ogst[(h * 8 + qb) % 2]
                    S.dma(gb[:], GS[qb * 512:(qb + 1) * 512, h * 128:(h + 1) * 128].rearrange("(j p) d -> p j d", p=128), reads=[GS], writes=[gb])
                    nb = 3 if kind == "A" else 2
                    for b in range(nb):
                        S.op(DVE, lambda e: e.tensor_copy(out=accsb[:, b, :], in_=pacc[:, b, :]), [pacc], [accsb])
                    for jq in range(4):
                        o0 = offs(jq)
                        S.op(DVE, lambda e: e.reciprocal(out=rz[:, 0:1], in_=accsf[:, o0 + 128:o0 + 129]), [accsb], [rz])
                        if kind == "A":
                            o1 = offs(4 + jq)
                            S.op(DVE, lambda e: e.reciprocal(out=rz[:, 1:2], in_=accsf[:, o1 + 128:o1 + 129]), [accsb], [rz])
                            S.op(DVE, lambda e: e.tensor_tensor(out=rz[:, 2:3], in0=rz[:, 1:2], in1=neglam[:, 2:3], op=ALU.mult), [rz, neglam], [rz])
                            S.op(DVE, lambda e: e.tensor_scalar(out=tmp[:], in0=accsf[:, o1:o1 + 128], scalar1=rz[:, 2:3], scalar2=None, op0=ALU.mult), [accsb, rz], [tmp])
                            S.op(DVE, lambda e: e.scalar_tensor_tensor(out=o4[:, jq, :], in0=accsf[:, o0:o0 + 128], scalar=rz[:, 0:1], in1=tmp[:], op0=ALU.mult, op1=ALU.add),
                                 [accsb, rz, tmp], [o4])
                            S.op(DVE, lambda e: e.scalar_tensor_tensor(out=junk[:], in0=o4[:, jq, :], scalar=1.0, in1=o4[:, jq, :], op0=ALU.mult, op1=ALU.mult,
                                                                       accum_out=ssj[:, jq:jq + 1]), [o4], [junk, ssj])
                        else:
                            S.op(DVE, lambda e: e.scalar_tensor_tensor(out=ogf[:, jq, :], in0=accsf[:, o0:o0 + 128], scalar=rz[:, 0:1], in1=gb[:, jq, :], op0=ALU.mult, op1=ALU.mult),
                                 [accsb, rz, gb], [ogf])
                    if kind == "A":
                        S.op(ACT, lambda e: e.activation(out=ssj[:, 4:8], in_=ssj[:, 0:4], func=AF.Ln, scale=1.0 / 128, bias=EPS), [ssj], [ssj])
                        S.op(ACT, lambda e: e.activation(out=ssj[:, 4:8], in_=ssj[:, 4:8], func=AF.Exp, scale=-0.5), [ssj], [ssj])
                        for jq in range(4):
                            S.op(DVE, lambda e: e.scalar_tensor_tensor(out=ogf[:, jq, :], in0=o4[:, jq, :], scalar=ssj[:, 4 + jq:5 + jq], in1=gb[:, jq, :], op0=ALU.mult, op1=ALU.mult),
                                 [o4, ssj, gb], [ogf])
                    for jq in range(4):
                        S.op(PE, lambda e: e.transpose(out=psO[:, jq * 128:(jq + 1) * 128], in_=ogf[:, jq, :], identity=ident[:]), [ogf, ident], [psO])
                    S.op(DVE, lambda e: e.tensor_copy(out=ob[:], in_=psO[:]), [psO], [ob])
                    S.dma(OGT[h, :, qb * 512:(qb + 1) * 512], ob[:], reads=[ob], writes=[OGT])

                load_head(0)
                pend = None
                for i, job in enumerate(jobs):
                    stage_S(i, job)
                    if pend is not None:
                        stage_PV(*pend)
                    pend = (i, job)
                    if job[1] == 0 and job[2] == 0 and job[0] + 1 < 8:
                        load_head(job[0] + 1)
                stage_PV(*pend)
                S.barrier()

        def phase_T3(w_out_ap, xsrc, xdst, rowscale=None):
            with ExitStack() as ph:
                Wo = k.sb(ph, [128, 8, DM], BF16, "Wo")
                psP = [k.ps(ph, [128, 512]) for _ in range(4)]
                if rowscale is not None:
                    stage = [k.sb(ph, [128, 2048], F32, "stg") for _ in range(2)]
                    load_w(ph, Wo, w_out_ap, 8, DM, gcol=rowscale[0], stage=stage, gfn=lambda c: rowscale[0][:, 0:1])
                else:
                    load_w(ph, Wo, w_out_ap, 8, DM)
                ogT = [k.sb(ph, [128, 8, 512], BF16, "ogT") for _ in range(2)]
                xb = [k.sb(ph, [128, 4, DM], F32, "xb") for _ in range(2)]
                pi = 0
                for tb_ in range(8):
                    b = tb_ % 2
                    tsl = slice(tb_ * 512, (tb_ + 1) * 512)
                    S.dma(ogT[b][:], OGT[:, :, tsl].rearrange("h p t -> p h t"), reads=[OGT], writes=[ogT[b]])
                    S.dma(xb[b][:], xsrc[tsl, :].rearrange("(j p) d -> p j d", p=128), reads=[xsrc], writes=[xb[b]])
                    for jq in range(4):
                        for ng in range(2):
                            bank = psP[pi % 4]
                            pi += 1
                            for c in range(8):
                                S.op(PE, lambda e: e.matmul(bank[:], lhsT=ogT[b][:, c, jq * 128:(jq + 1) * 128], rhs=Wo[:, c, ng * 512:(ng + 1) * 512],
                                                            start=(c == 0), stop=(c == 7)), [ogT[b], Wo], [bank])
                            S.op(DVE, lambda e: e.tensor_tensor(out=xb[b][:, jq, ng * 512:(ng + 1) * 512], in0=bank[:], in1=xb[b][:, jq, ng * 512:(ng + 1) * 512], op=ALU.add),
                                 [bank, xb[b]], [xb[b]])
                    S.dma(xdst[tsl, :].rearrange("(j p) d -> p j d", p=128), xb[b][:], reads=[xb[b]], writes=[xdst])
                S.barrier()

        def subg_scale(ph_stack, j, lambda_init):
            sg = k.sb(ph_stack, [128, 1], F32, "subg")
            pst = k.ps(ph_stack, [128, 512])
            cv = colvec(ph_stack, pst, A["sub_g"][j:j + 1, :], 1)
            S.op(DVE, lambda e: e.tensor_scalar(out=sg[:], in0=cv[:], scalar1=float(1.0 - lambda_init), scalar2=None, op0=ALU.mult), [cv], [sg])
            return sg

        def phase_T1_B(xsrc):
            with ExitStack() as ph:
                W = k.sb(ph, [128, 8, 3072], BF16, "W")
                stage = [k.sb(ph, [128, 2048], F32, "stg") for _ in range(2)]
                psT = [k.ps(ph, [128, 512]) for _ in range(2)]
                psP = [k.ps(ph, [128, 512]) for _ in range(4)]
                psQ = [k.ps(ph, [128, 512]) for _ in range(2)]
                gcol = colvec(ph, psT[0], Bp["norm_g"][0:1, :].rearrange("o (c p) -> (o c) p", p=128), 8)
                load_w(ph, W, Bp["w_in"][0], 8, 3072, gcol=gcol, stage=stage, gfn=lambda c: gcol[:, c:c + 1])
                brow = k.sb(ph, [128, 3072], F32, "brow")
                S.dma(brow[:], Bp["b_in"][0:1, :].partition_broadcast(128), writes=[brow])
                front.brow = brow
                xt = [k.sb(ph, [128, DM], F32, "xt") for _ in range(2)]
                stt_ = [k.sb(ph, [128, 4], F32, "stt") for _ in range(2)]
                xT = [k.sb(ph, [128, 8, 128], BF16, "xT") for _ in range(2)]
                front.junk = k.sb(ph, [128, DM], F32, "junk")
                af = k.sb(ph, [128, DM], F32, "af")
                e_ = [k.sb(ph, [128, 512], F32, "e") for _ in range(2)]
                gfb = [k.sb(ph, [128, 512], F32, "gf") for _ in range(2)]
                u = k.sb(ph, [128, DM], F32, "u")
                gg = k.sb(ph, [128, DM], F32, "gg")
                uTst = [k.sb(ph, [128, 8, 128], BF16, "uTst") for _ in range(2)]
                gTst = [k.sb(ph, [128, 8, 128], BF16, "gTst") for _ in range(2)]
                pi = 0
                for t in range(NT):
                    b2 = t % 2
                    front.stt_cur = stt_[b2]
                    rstd, nrstd = front(t, xsrc, xt[b2], stt_[b2], xT[b2], psT)
                    sc = stt_[b2]
                    tsl = slice(t * 128, (t + 1) * 128)
                    for fg in range(6):
                        bank = psP[pi % 4]
                        pi += 1
                        project(xT[b2], W, fg * 512, 512, bank)
                        h0 = (fg % 2) * 512
                        if fg < 2:
                            S.op(DVE, lambda e: e.scalar_tensor_tensor(out=af[:, h0:h0 + 512], in0=bank[:], scalar=rstd, in1=brow[:, fg * 512:(fg + 1) * 512],
                                                                       op0=ALU.mult, op1=ALU.add), [bank, sc, brow], [af])
                        elif fg < 4:
                            gf = gfb[fg % 2]
                            ee = e_[fg % 2]
                            S.op(DVE, lambda e: e.scalar_tensor_tensor(out=gf[:], in0=bank[:], scalar=rstd, in1=brow[:, fg * 512:(fg + 1) * 512],
                                                                       op0=ALU.mult, op1=ALU.add), [bank, sc, brow], [gf])
                            S.op(ACT, lambda e: e.activation(out=ee[:], in_=gf[:], func=AF.Exp, scale=-1.0), [gf], [ee])
                            S.op(DVE, lambda e: e.tensor_scalar(out=ee[:], in0=ee[:], scalar1=1.0, scalar2=None, op0=ALU.add), [ee], [ee])
                            S.op(DVE, lambda e: e.reciprocal(out=ee[:], in_=ee[:]), [ee], [ee])
                            S.op(DVE, lambda e: e.tensor_tensor(out=u[:, h0:h0 + 512], in0=af[:, h0:h0 + 512], in1=ee[:], op=ALU.mult), [af, ee], [u])
                        else:
                            silu_gate(bank, rstd, nrstd, e_[fg % 2], gg[:, h0:h0 + 512], gg, bias_row=brow[:, fg * 512:(fg + 1) * 512], gf=gfb[fg % 2])
                    transp_store(u, 8, 128, psQ, uTst[b2], 0, u)
                    transp_store(gg, 8, 128, psQ, gTst[b2], 0, gg)
                    S.dma(QT[:, :, tsl].rearrange("h p t -> p h t"), uTst[b2][:], reads=[uTst[b2]], writes=[QT])
                    S.dma(KT[:, :, tsl].rearrange("h p t -> p h t"), gTst[b2][:], reads=[gTst[b2]], writes=[KT])
                S.barrier()

        def phase_T2_B(xsrc, xdst):
            with ExitStack() as ph:
                psC = [k.ps(ph, [128, 512]) for _ in range(2)]
                psS1 = k.ps(ph, [128, 512])
                psS2 = k.ps(ph, [128, 512])
                psP = [k.ps(ph, [128, 512]) for _ in range(3)]
                psX = k.ps(ph, [128, 512])
                Wo = k.sb(ph, [128, 8, DM], BF16, "Wo")
                load_w(ph, Wo, Bp["w_out"][0], 8, DM)
                cw = k.sb(ph, [31, DM], F32, "cw")
                S.dma(cw[:], Bp["conv_w"][0], writes=[cw])
                wcol = k.sb(ph, [128, 8, 31], F32, "wcol")
                for ct in range(8):
                    S.op(PE, lambda e: e.transpose(out=psX[:, ct * 32:ct * 32 + 31], in_=cw[:, ct * 128:(ct + 1) * 128], identity=ident[0:31, 0:31]), [cw, ident], [psX])
                S.op(DVE, lambda e: e.tensor_copy(out=wcol[:], in_=psX[:, 0:256].rearrange("p (a b) -> p a b", a=8)[:, :, 0:31]), [psX], [wcol])
                cb = colvec(ph, psX, Bp["conv_b"][0:1, :].rearrange("o (c p) -> (o c) p", p=128), 8)
                lng = colvec(ph, psX, Bp["ln_g"][0:1, :].rearrange("o (c p) -> (o c) p", p=128), 8)
                lnb = colvec(ph, psX, Bp["ln_b"][0:1, :].rearrange("o (c p) -> (o c) p", p=128), 8)
                diag = k.sb(ph, [128, 31, 8, 128], BF16, "diag")
                for kk in range(31):
                    for ct in range(8):
                        eng = DVE if (kk * 8 + ct) % 3 else POOL
                        S.op(eng, lambda e: e.tensor_scalar(out=diag[:, kk, ct, :], in0=ident[:], scalar1=wcol[:, ct, kk:kk + 1], scalar2=None, op0=ALU.mult),
                             [ident, wcol], [diag])
                ones = k.sb(ph, [128, 128], F32, "ones")
                S.op(DVE, lambda e: e.memset(ones[:], 1.0), [], [ones])
                UTb = k.sb(ph, [128, 8, 542], BF16, "UTb")
                GTb = k.sb(ph, [128, 8, 512], BF16, "GTb")
                xb = k.sb(ph, [128, 4, DM], F32, "xb")
                v = k.sb(ph, [128, 8, 512], F32, "v")
                sqt = [k.sb(ph, [128, 512], F32, "sqt") for _ in range(2)]
                mean = k.sb(ph, [128, 512], F32, "mean")
                m2 = k.sb(ph, [128, 512], F32, "m2")
                rstdb = k.sb(ph, [128, 512], F32, "rstdb")
                tt = [k.sb(ph, [128, 512], F32, "tt") for _ in range(2)]
                ee = [k.sb(ph, [128, 512], F32, "ee") for _ in range(2)]
                zg = k.sb(ph, [128, 8, 512], BF16, "zg")
                ci = 0
                pi = 0
                for tb_ in range(8):
                    tsl = slice(tb_ * 512, (tb_ + 1) * 512)
                    if tb_ == 0:
                        S.op(DVE, lambda e: e.memset(UTb[:, :, 0:30], 0.0), [], [UTb])
                        S.dma(UTb[:, :, 30:542], QT[:, :, 0:512].rearrange("h p t -> p h t"), reads=[QT], writes=[UTb])
                    else:
                        S.dma(UTb[:], QT[:, :, tb_ * 512 - 30:(tb_ + 1) * 512].rearrange("h p t -> p h t"), reads=[QT], writes=[UTb])
                    S.dma(GTb[:], KT[:, :, tsl].rearrange("h p t -> p h t"), reads=[KT], writes=[GTb])
                    S.dma(xb[:], xsrc[tsl, :].rearrange("(j p) d -> p j d", p=128), reads=[xsrc], writes=[xb])
                    for ct in range(8):
                        bank = psC[ci % 2]
                        ci += 1
                        for kk in range(31):
                            S.op(PE, lambda e: e.matmul(bank[:], lhsT=diag[:, kk, ct, :], rhs=UTb[:, ct, kk:kk + 512], start=(kk == 0), stop=(kk == 30)),
                                 [diag, UTb], [bank])
                        S.op(ACT, lambda e: e.activation(out=v[:, ct, :], in_=bank[:], func=AF.Identity, bias=cb[:, ct:ct + 1]), [bank, cb], [v])
                        sq_ = sqt[ct % 2]
                        S.op(ACT, lambda e: e.activation(out=sq_[:], in_=v[:, ct, :], func=AF.Square), [v], [sq_])
                        S.op(PE, lambda e: e.matmul(psS1[:], lhsT=ones[:], rhs=v[:, ct, :], start=(ct == 0), stop=(ct == 7)), [ones, v], [psS1])
                        S.op(PE, lambda e: e.matmul(psS2[:], lhsT=ones[:], rhs=sq_[:], start=(ct == 0), stop=(ct == 7)), [ones, sq_], [psS2])
                    S.op(ACT, lambda e: e.activation(out=mean[:], in_=psS1[:], func=AF.Identity, scale=1.0 / DM), [psS1], [mean])
                    S.op(DVE, lambda e: e.tensor_tensor(out=m2[:], in0=mean[:], in1=mean[:], op=ALU.mult), [mean], [m2])
                    S.op(DVE, lambda e: e.scalar_tensor_tensor(out=m2[:], in0=psS2[:], scalar=1.0 / DM, in1=m2[:], op0=ALU.mult, op1=ALU.subtract), [psS2, m2], [m2])
                    S.op(ACT, lambda e: e.activation(out=m2[:], in_=m2[:], func=AF.Ln, bias=EPS), [m2], [m2])
                    S.op(ACT, lambda e: e.activation(out=rstdb[:], in_=m2[:], func=AF.Exp, scale=-0.5), [m2], [rstdb])
                    for ct in range(8):
                        t_ = tt[ct % 2]
                        e2 = ee[ct % 2]
                        S.op(DVE, lambda e: e.tensor_tensor(out=t_[:], in0=v[:, ct, :], in1=mean[:], op=ALU.subtract), [v, mean], [t_])
                        S.op(DVE, lambda e: e.tensor_tensor(out=t_[:], in0=t_[:], in1=rstdb[:], op=ALU.mult), [t_, rstdb], [t_])
                        S.op(DVE, lambda e: e.tensor_scalar(out=t_[:], in0=t_[:], scalar1=lng[:, ct:ct + 1], scalar2=lnb[:, ct:ct + 1], op0=ALU.mult, op1=ALU.add),
                             [t_, lng, lnb], [t_])
                        S.op(ACT, lambda e: e.activation(out=e2[:], in_=t_[:], func=AF.Exp, scale=-1.0), [t_], [e2])
                        S.op(DVE, lambda e: e.tensor_scalar(out=e2[:], in0=e2[:], scalar1=1.0, scalar2=None, op0=ALU.add), [e2], [e2])
                        S.op(DVE, lambda e: e.reciprocal(out=e2[:], in_=e2[:]), [e2], [e2])
                        S.op(DVE, lambda e: e.tensor_tensor(out=t_[:], in0=t_[:], in1=e2[:], op=ALU.mult), [t_, e2], [t_])
                        S.op(DVE, lambda e: e.tensor_tensor(out=zg[:, ct, :], in0=t_[:], in1=GTb[:, ct, :], op=ALU.mult), [t_, GTb], [zg])
                    for jq in range(4):
                        for ng in range(2):
                            bank = psP[pi % 3]
                            pi += 1
                            for c in range(8):
                                S.op(PE, lambda e: e.matmul(bank[:], lhsT=zg[:, c, jq * 128:(jq + 1) * 128], rhs=Wo[:, c, ng * 512:(ng + 1) * 512],
                                                            start=(c == 0), stop=(c == 7)), [zg, Wo], [bank])
                            S.op(DVE, lambda e: e.tensor_tensor(out=xb[:, jq, ng * 512:(ng + 1) * 512], in0=bank[:], in1=xb[:, jq, ng * 512:(ng + 1) * 512], op=ALU.add),
                                 [bank, xb], [xb])
                    S.dma(xdst[tsl, :].rearrange("(j p) d -> p j d", p=128), xb[:], reads=[xb], writes=[xdst])
                S.barrier()

        S.barrier()
        cur = XIN
        for li in range(nlayers):
            kind, j = li % 3, li // 3
            if kinds is not None:
                kind, j = kinds[li], 0
            dst = OUT if li == nlayers - 1 else XS[li % 2]
            if kind == 0:
                lam0 = 0.8 - 0.6 * math.exp(-0.3 * li)
                phase_T1_attn("A", j, cur)
                phase_T2_attn("A", j, lambda_init=lam0)
                with ExitStack() as ph0:
                    sg = subg_scale(ph0, j, lam0)
                    phase_T3(A["w_out"][j], cur, dst, rowscale=(sg,))
            elif kind == 1:
                phase_T1_B(cur)
                phase_T2_B(cur, dst)
            else:
                phase_T1_attn("C", j, cur)
                phase_T2_attn("C", j)
                phase_T3(C["w_out"][j], cur, dst)
            cur = dst
        if dbg_out:
            for nm, ap_ in dbg_out.items():
                src = {"qt": QT, "kt": KT, "vs": VS, "gs": GS, "ogt": OGT, "qr": QR, "kr": KR}[nm]
                S.dma(ap_, src.t, reads=[src], writes=[OUT])
        for e2, c in sorted(OUT.writers.items()):
            nc.sync.wait_ge(S.sems[e2], S._semval(e2, c))
        build_program.stats = (S.ninst, S.nwaits)
    return nc


def _consts():
    inv = 1.0 / (10000.0 ** (np.arange(0, 64, 2, dtype=np.float32) / np.float32(64)))
    ang = np.arange(SEQ, dtype=np.float32)[:, None] * inv.astype(np.float32)[None, :]
    ang = ang.astype(np.float32)
    c, s = np.cos(ang).astype(np.float32), np.sin(ang).astype(np.float32)
    cs4 = np.concatenate([c, s, c, s], axis=1).astype(np.float32)
    return np.eye(128, dtype=np.float32), np.ascontiguousarray(cs4)


_NC_CACHE = {}


def kernel(**inputs):
    x = np.ascontiguousarray(np.asarray(inputs["x"], dtype=np.float32))
    if "nc" not in _NC_CACHE:
        _NC_CACHE["nc"] = build_program()
    nc = _NC_CACHE["nc"]
    ident, cs4 = _consts()
    shared = {kname: np.ascontiguousarray(np.asarray(v, dtype=np.float32)) for kname, v in inputs.items() if kname != "x"}
    shared["ident"] = ident
    shared["cs4"] = cs4
    in_maps = []
    for c in range(NCORES):
        m = dict(shared)
        m["x"] = x[c]
        in_maps.append(m)
    res = run_bass_kernel_spmd(nc, in_maps, core_ids=list(range(NCORES)))
    return np.stack([np.asarray(r["out"], dtype=np.float32) for r in res.results], axis=0)
```

```python
import math
from contextlib import ExitStack

import numpy as np
import concourse.bass as bass
import concourse.mybir as mybir
from concourse.bass_utils import run_bass_kernel_spmd

F32 = mybir.dt.float32
BF16 = mybir.dt.bfloat16
AF = mybir.ActivationFunctionType
ALU = mybir.AluOpType
AX = mybir.AxisListType

PE, ACT, DVE, POOL, SP = 0, 1, 2, 3, 4
NDMA = 12

SEQ, DM, NT = 4096, 1024, 32
EPS = 1e-6
NCORES = 8


class Buf:
    def __init__(self, t, name, disjoint=False):
        self.t = t
        self.name = name
        self.writers = {}
        self.readers = {}
        self.disjoint = disjoint

    def __getitem__(self, k):
        return self.t[k]


class Sched:
    def __init__(self, nc, stack):
        self.nc = nc
        self.eng = [nc.tensor, nc.scalar, nc.vector, nc.gpsimd, nc.sync]
        self.nE = 5 + NDMA
        self.sems = [stack.enter_context(nc.semaphore("s%d" % i)) for i in range(self.nE)]
        self.count = [0] * self.nE
        self.clk = [[0] * self.nE for _ in range(self.nE)]
        self.hist = [[None] for _ in range(self.nE)]
        self.dma_rr = 0
        self.nwaits = 0
        self.ninst = 0

    def _semval(self, e, c):
        return c * 16 if e >= 5 else c

    def _deps(self, reads, writes):
        deps = {}
        for t in reads:
            for e, c in t.writers.items():
                if deps.get(e, 0) < c:
                    deps[e] = c
        for t in writes:
            for e, c in t.readers.items():
                if deps.get(e, 0) < c:
                    deps[e] = c
            if not (t.disjoint and not t.readers):
                for e, c in t.writers.items():
                    if deps.get(e, 0) < c:
                        deps[e] = c
        return deps

    def _emit_waits(self, q, deps):
        clk = self.clk[q]
        h = self.eng[q]
        for e2, c in sorted(deps.items()):
            if e2 == q:
                if q == PE:
                    continue
                if c < self.count[q] - 1:
                    continue
            if c <= clk[e2]:
                continue
            h.wait_ge(self.sems[e2], self._semval(e2, c))
            self.nwaits += 1
            hv = self.hist[e2][c]
            for k in range(self.nE):
                if hv[k] > clk[k]:
                    clk[k] = hv[k]
            if clk[e2] < c:
                clk[e2] = c

    def _commit(self, e, ins, reads, writes, snap):
        self.count[e] += 1
        c = self.count[e]
        ins.then_inc(self.sems[e], 16 if e >= 5 else 1)
        self.ninst += 1
        snap = list(snap)
        if e == PE:
            snap[e] = c
        self.hist[e].append(snap)
        for t in reads:
            if t.readers.get(e, 0) < c:
                t.readers[e] = c
        for t in writes:
            if t.readers or not t.disjoint:
                t.writers = {e: c}
                t.readers = {}
            else:
                t.writers[e] = c

    def op(self, e, fn, reads=(), writes=()):
        self._emit_waits(e, self._deps(reads, writes))
        ins = fn(self.eng[e])
        self._commit(e, ins, reads, writes, self.clk[e])
        return ins

    def dma(self, out, in_, reads=(), writes=(), q=SP, **kw):
        j = 5 + self.dma_rr
        self.dma_rr = (self.dma_rr + 1) % NDMA
        deps = self._deps(reads, writes)
        if self.count[j] > 0:
            deps[j] = max(deps.get(j, 0), self.count[j])
        self._emit_waits(q, deps)
        ins = self.eng[q].dma_start(out=out, in_=in_, **kw)
        self._commit(j, ins, reads, writes, self.clk[q])
        return ins

    def barrier(self):
        tot = {e: self.count[e] for e in range(self.nE) if self.count[e] > 0}
        for q in range(5):
            clk = self.clk[q]
            h = self.eng[q]
            for e2, c in sorted(tot.items()):
                if e2 == q and q == PE:
                    continue
                if c <= clk[e2]:
                    continue
                h.wait_ge(self.sems[e2], self._semval(e2, c))
                self.nwaits += 1
                clk[e2] = c


class K:
    def __init__(self, nc, st):
        self.nc = nc
        self.st = st
        self.S = Sched(nc, st)
        self.n = 0

    def sb(self, stack, shape, dt, name=None):
        self.n += 1
        nm = "%s_%d" % (name or "t", self.n)
        return Buf(stack.enter_context(self.nc.sbuf_tensor(nm, list(shape), dt)), nm)

    def ps(self, stack, shape, dt=F32, name=None):
        self.n += 1
        nm = "%s_%d" % (name or "p", self.n)
        return Buf(stack.enter_context(self.nc.psum_tensor(nm, list(shape), dt)), nm)


def bc(ap, shape):
    return ap.unsqueeze(1).broadcast_to(list(shape))


def build_program(nlayers=4, dbg=None, kinds=None):
    nc = bass.Bass("TRN2", target_bir_lowering=False)
    names = {}

    def din(name, shape):
        names[name] = nc.dram_tensor(name, list(shape), F32, kind="ExternalInput").ap()
        return names[name]

    x_in = din("x", [SEQ, DM])
    ident_d = din("ident", [128, 128])
    cs4_d = din("cs4", [SEQ, 128])
    A = dict(norm_g=din("a_norm_g", [2, DM]), w_in=din("a_w_in", [2, DM, 4096]), q_g=din("a_q_norm_g", [2, 64]),
             k_g=din("a_k_norm_g", [2, 64]), lq1=din("a_lam_q1", [2, 64]), lk1=din("a_lam_k1", [2, 64]),
             lq2=din("a_lam_q2", [2, 64]), lk2=din("a_lam_k2", [2, 64]), sub_g=din("a_sub_norm_g", [2, 128]),
             w_out=din("a_w_out", [2, DM, DM]))
    Bp = dict(norm_g=din("b_norm_g", [1, DM]), w_in=din("b_w_in", [1, DM, 3072]), b_in=din("b_b_in", [1, 3072]),
              conv_w=din("b_conv_w", [1, 31, DM]), conv_b=din("b_conv_b", [1, DM]), ln_g=din("b_ln_g", [1, DM]),
              ln_b=din("b_ln_b", [1, DM]), w_out=din("b_w_out", [1, DM, DM]))
    C = dict(norm_g=din("c_norm_g", [1, DM]), w_in=din("c_w_in", [1, DM, 1472]), cq_g=din("c_cq_norm_g", [1, 256]),
             w_uq=din("c_w_uq", [1, 256, 1536]), ckv_g=din("c_ckv_norm_g", [1, 128]), w_ukv=din("c_w_ukv", [1, 128, 2048]),
             q_g=din("c_q_norm_g", [1, 192]), k_g=din("c_k_norm_g", [1, 192]), w_out=din("c_w_out", [1, DM, DM]))
    out_d = nc.dram_tensor("out", [SEQ, DM], F32, kind="ExternalOutput").ap()

    def scratch(name, shape, dt):
        return Buf(nc.dram_tensor(name, list(shape), dt, kind="Internal").ap(), name, disjoint=True)

    XS = [scratch("xs0", [SEQ, DM], F32), scratch("xs1", [SEQ, DM], F32)]
    QT = scratch("qt", [8, 128, SEQ], BF16)
    KT = scratch("kt", [8, 128, SEQ], BF16)
    QR = scratch("qr", [8, 64, SEQ], BF16)
    KR = scratch("kr", [64, SEQ], BF16)
    VS = scratch("vs", [SEQ, 8, 130], BF16)
    GS = scratch("gs", [SEQ, DM], BF16)
    OGT = scratch("ogt", [8, 128, SEQ], BF16)
    OUT = Buf(out_d, "out", disjoint=True)
    XIN = Buf(x_in, "xin", disjoint=True)
    dbg_out = None
    if dbg is not None:
        dbg_out = {}
        for nm, shp, dt in dbg:
            dbg_out[nm] = nc.dram_tensor("dbg_" + nm, list(shp), dt, kind="ExternalOutput").ap()

    with ExitStack() as st:
        k = K(nc, st)
        S = k.S
        ident = k.sb(st, [128, 128], F32, "ident")
        S.dma(ident[:], ident_d, writes=[ident])

        def colvec(ph, psb, src2d, n):
            dst = k.sb(ph, [128, n], F32, "cv")
            S.dma(dst[:], src2d.rearrange("c p -> p c"), writes=[dst], allow_slow_non_contiguous=True)
            return dst

        def load_w(ph, dst, src, nch, F, gcol=None, stage=None, gfn=None):
            if gcol is None:
                S.dma(dst[:], src.rearrange("(c p) f -> p c f", p=128), writes=[dst], q=POOL)
                return
            FH = min(F, 2048)
            i = 0
            for c in range(nch):
                for f0 in range(0, F, FH):
                    sg = stage[i % len(stage)]
                    i += 1
                    fw = min(FH, F - f0)
                    S.dma(sg[:, 0:fw], src[c * 128:(c + 1) * 128, f0:f0 + fw], writes=[sg])
                    sc = gfn(c)
                    sel = i % 4
                    if sel == 0:
                        S.op(POOL, lambda e: e.tensor_scalar(out=dst[:, c, f0:f0 + fw], in0=sg[:, 0:fw], scalar1=sc, scalar2=None,
                                                              op0=ALU.mult), [sg, gcol], [dst])
                    elif sel == 2:
                        S.op(ACT, lambda e: e.activation(out=dst[:, c, f0:f0 + fw], in_=sg[:, 0:fw], func=AF.Identity, scale=sc), [sg, gcol], [dst])
                    else:
                        S.op(DVE, lambda e: e.tensor_scalar(out=dst[:, c, f0:f0 + fw], in0=sg[:, 0:fw], scalar1=sc, scalar2=None,
                                                             op0=ALU.mult), [sg, gcol], [dst])

        def front(t, xsrc, xt, stt_, xT, psT, ng=8):
            S.dma(xt[:], xsrc[t * 128:(t + 1) * 128, :], reads=[xsrc], writes=[xt])
            junk = front.junk
            S.op(ACT, lambda e: e.activation(out=junk[:], in_=xt[:], func=AF.Square, accum_out=stt_[:, 0:1]), [xt], [junk, stt_])
            S.op(ACT, lambda e: e.activation(out=stt_[:, 1:2], in_=stt_[:, 0:1], func=AF.Ln, scale=1.0 / DM, bias=EPS), [stt_], [stt_])
            S.op(ACT, lambda e: e.activation(out=stt_[:, 2:3], in_=stt_[:, 1:2], func=AF.Exp, scale=-0.5), [stt_], [stt_])
            S.op(DVE, lambda e: e.tensor_scalar(out=stt_[:, 3:4], in0=stt_[:, 2:3], scalar1=-1.0, scalar2=None, op0=ALU.mult), [stt_], [stt_])
            for hb in range(2):
                for i in range(4):
                    c = hb * 4 + i
                    S.op(PE, lambda e: e.transpose(out=psT[hb][:, i * 128:(i + 1) * 128], in_=xt[:, c * 128:(c + 1) * 128], identity=ident[:]),
                         [xt, ident], [psT[hb]])
                S.op(DVE, lambda e: e.tensor_copy(out=xT[:, hb * 4:(hb + 1) * 4, :], in_=psT[hb][:].rearrange("p (a b) -> p a b", a=4)),
                     [psT[hb]], [xT])
            return stt_[:, 2:3], stt_[:, 3:4]

        def project(xT, W, f0, fw, bank, nch=8):
            for c in range(nch):
                S.op(PE, lambda e: e.matmul(bank[:, 0:fw], lhsT=xT[:, c, :], rhs=W[:, c, f0:f0 + fw], start=(c == 0), stop=(c == nch - 1)),
                     [xT, W], [bank])

        def silu_gate(bank, rstd, nrstd, e_, out_ap, outbuf, bias_row=None, gf=None, width=512):
            if bias_row is None:
                S.op(ACT, lambda e: e.activation(out=e_[:, 0:width], in_=bank[:, 0:width], func=AF.Exp, scale=nrstd), [bank, front.stt_cur], [e_])
            else:
                S.op(DVE, lambda e: e.scalar_tensor_tensor(out=gf[:, 0:width], in0=bank[:, 0:width], scalar=rstd, in1=bias_row, op0=ALU.mult, op1=ALU.add),
                     [bank, front.stt_cur, front.brow], [gf])
                S.op(ACT, lambda e: e.activation(out=e_[:, 0:width], in_=gf[:, 0:width], func=AF.Exp, scale=-1.0), [gf], [e_])
            S.op(DVE, lambda e: e.tensor_scalar(out=e_[:, 0:width], in0=e_[:, 0:width], scalar1=1.0, scalar2=None, op0=ALU.add), [e_], [e_])
            S.op(DVE, lambda e: e.reciprocal(out=e_[:, 0:width], in_=e_[:, 0:width]), [e_], [e_])
            if bias_row is None:
                S.op(DVE, lambda e: e.scalar_tensor_tensor(out=out_ap, in0=bank[:, 0:width], scalar=rstd, in1=e_[:, 0:width], op0=ALU.mult, op1=ALU.mult),
                     [bank, front.stt_cur, e_], [outbuf])
            else:
                S.op(DVE, lambda e: e.tensor_tensor(out=out_ap, in0=gf[:, 0:width], in1=e_[:, 0:width], op=ALU.mult), [gf, e_], [outbuf])

        def group_rs(src, ng, gs, sq, ssg, rs, off=0):
            n = ng * gs
            S.op(DVE, lambda e: e.tensor_tensor(out=sq[:, 0:n], in0=src, in1=src, op=ALU.mult), [group_rs.srcbuf], [sq])
            S.op(DVE, lambda e: e.tensor_reduce(out=ssg[:, off:off + ng], in_=sq[:, 0:n].rearrange("p (g d) -> p g d", g=ng), axis=AX.X, op=ALU.add),
                 [sq], [ssg])
            S.op(ACT, lambda e: e.activation(out=ssg[:, off:off + ng], in_=ssg[:, off:off + ng], func=AF.Ln, scale=1.0 / gs, bias=EPS), [ssg], [ssg])
            S.op(ACT, lambda e: e.activation(out=rs[:, off:off + ng], in_=ssg[:, off:off + ng], func=AF.Exp, scale=-0.5), [ssg], [rs])

        def rope(n3, T, ro3, ta, tb, ng, nbuf, Tbuf, robuf):
            n1, n2 = n3[:, :, 0:32], n3[:, :, 32:64]
            C1, S2, C2, S1 = [bc(T[:, i * 32:(i + 1) * 32], [128, ng, 32]) for i in range(4)]
            a3 = ta[:, 0:ng * 32].rearrange("p (g d) -> p g d", g=ng)
            b3 = tb[:, 0:ng * 32].rearrange("p (g d) -> p g d", g=ng)
            S.op(DVE, lambda e: e.tensor_tensor(out=a3, in0=n1, in1=C1, op=ALU.mult), [nbuf, Tbuf], [ta])
            S.op(DVE, lambda e: e.tensor_tensor(out=b3, in0=n2, in1=S2, op=ALU.mult), [nbuf, Tbuf], [tb])
            S.op(DVE, lambda e: e.tensor_tensor(out=ro3[:, :, 0:32], in0=a3, in1=b3, op=ALU.subtract), [ta, tb], [robuf])
            S.op(DVE, lambda e: e.tensor_tensor(out=a3, in0=n2, in1=C2, op=ALU.mult), [nbuf, Tbuf], [ta])
            S.op(DVE, lambda e: e.tensor_tensor(out=b3, in0=n1, in1=S1, op=ALU.mult), [nbuf, Tbuf], [tb])
            S.op(DVE, lambda e: e.tensor_tensor(out=ro3[:, :, 32:64], in0=a3, in1=b3, op=ALU.add), [ta, tb], [robuf])

        def transp_store(src, nblk, bw, psQ, stg, stg_off, srcbuf, evac=ACT):
            i = 0
            qi = transp_store.qi
            while i < nblk:
                nb = min(4, nblk - i)
                bank = psQ[qi % 2]
                qi += 1
                for b in range(nb):
                    S.op(PE, lambda e: e.transpose(out=bank[0:bw, b * 128:(b + 1) * 128], in_=src[:, (i + b) * bw:(i + b + 1) * bw], identity=ident[:]),
                         [srcbuf, ident], [bank])
                if evac == ACT:
                    S.op(ACT, lambda e: e.activation(out=stg[0:bw, stg_off + i:stg_off + i + nb, :], in_=bank[0:bw, 0:nb * 128].rearrange("p (a b) -> p a b", a=nb),
                                                      func=AF.Copy), [bank], [stg])
                else:
                    S.op(DVE, lambda e: e.tensor_copy(out=stg[0:bw, stg_off + i:stg_off + i + nb, :], in_=bank[0:bw, 0:nb * 128].rearrange("p (a b) -> p a b", a=nb)),
                         [bank], [stg])
                i += nb
            transp_store.qi = qi
        transp_store.qi = 0

        def phase_T1_attn(kind, j, xsrc, lambda_init=None):
            with ExitStack() as ph:
                FW = 4096 if kind == "A" else 1472
                P = A if kind == "A" else C
                W = k.sb(ph, [128, 8, FW], BF16, "W")
                stage = [k.sb(ph, [128, 2048], F32, "stg") for _ in range(3)]
                psT = [k.ps(ph, [128, 512]) for _ in range(2)]
                psP = [k.ps(ph, [128, 512]) for _ in range(4)]
                psQ = [k.ps(ph, [128, 512]) for _ in range(2)]
                gcol = colvec(ph, psT[0], P["norm_g"][j:j + 1, :].rearrange("o (c p) -> (o c) p", p=128), 8)
                load_w(ph, W, P["w_in"][j], 8, FW, gcol=gcol, stage=stage, gfn=lambda c: gcol[:, c:c + 1])
                cs4 = k.sb(ph, [128, NT, 128], F32, "cs4")
                S.dma(cs4[:], cs4_d.rearrange("(t p) d -> p t d", p=128), writes=[cs4])
                gq = k.sb(ph, [128, 64], F32, "gq")
                gk = k.sb(ph, [128, 64], F32, "gk")
                G4q = k.sb(ph, [128, 128], F32, "G4q")
                G4k = k.sb(ph, [128, 128], F32, "G4k")
                if kind == "A":
                    S.dma(gq[:], P["q_g"][j:j + 1, :].partition_broadcast(128), writes=[gq])
                    S.dma(gk[:], P["k_g"][j:j + 1, :].partition_broadcast(128), writes=[gk])
                    qscale = 64 ** -0.5
                else:
                    S.dma(gq[:], P["q_g"][j:j + 1, 128:192].partition_broadcast(128), writes=[gq])
                    S.dma(gk[:], P["k_g"][j:j + 1, 128:192].partition_broadcast(128), writes=[gk])
                    qscale = 192 ** -0.5
                for (g_, G4, sc) in ((gq, G4q, qscale), (gk, G4k, 1.0)):
                    for i, (a0, a1) in enumerate(((0, 32), (32, 64), (32, 64), (0, 32))):
                        S.op(DVE, lambda e: e.tensor_scalar(out=G4[:, i * 32:(i + 1) * 32], in0=g_[:, a0:a1], scalar1=sc, scalar2=None, op0=ALU.mult), [g_], [G4])
                xt = [k.sb(ph, [128, DM], F32, "xt") for _ in range(2)]
                stt_ = [k.sb(ph, [128, 4], F32, "stt") for _ in range(2)]
                xT = [k.sb(ph, [128, 8, 128], BF16, "xT") for _ in range(2)]
                front.junk = k.sb(ph, [128, DM], F32, "junk")
                e_ = [k.sb(ph, [128, 512], F32, "e") for _ in range(2)]
                Gst = [k.sb(ph, [128, DM], BF16, "Gst") for _ in range(2)]
                Vst = [k.sb(ph, [128, 8, 130], BF16, "Vst") for _ in range(2)]
                for v in Vst:
                    S.op(DVE, lambda e: e.memset(v[:, :, 128:130], 1.0), [], [v])
                TQ = k.sb(ph, [128, 128], F32, "TQ")
                TK = k.sb(ph, [128, 128], F32, "TK")
                ta = k.sb(ph, [128, 512], F32, "ta")
                tb = k.sb(ph, [128, 512], F32, "tb")
                if kind == "A":
                    qf = k.sb(ph, [128, 2048], F32, "qf")
                    sq = k.sb(ph, [128, 2048], F32, "sq")
                    ssg = k.sb(ph, [128, 32], F32, "ssg")
                    rs = k.sb(ph, [128, 32], F32, "rs")
                    ro = k.sb(ph, [128, 2048], F32, "ro")
                    qTst = [k.sb(ph, [128, 16, 128], BF16, "qTst") for _ in range(2)]
                else:
                    Wuq = k.sb(ph, [128, 2, 1536], BF16, "Wuq")
                    Wukv = k.sb(ph, [128, 1, 2048], BF16, "Wukv")
                    cqg = colvec(ph, psT[0], P["cq_g"][j:j + 1, :].rearrange("o (c p) -> (o c) p", p=128), 2)
                    ckvg = colvec(ph, psT[1], P["ckv_g"][j:j + 1, :].rearrange("o (c p) -> (o c) p", p=128), 1)
                    load_w(ph, Wuq, P["w_uq"][j], 2, 1536, gcol=cqg, stage=stage, gfn=lambda c: cqg[:, c:c + 1])
                    load_w(ph, Wukv, P["w_ukv"][j], 1, 2048, gcol=ckvg, stage=stage, gfn=lambda c: ckvg[:, c:c + 1])
                    gqn = k.sb(ph, [128, 128], F32, "gqn")
                    gkn = k.sb(ph, [128, 128], F32, "gkn")
                    S.dma(gqn[:], P["q_g"][j:j + 1, 0:128].partition_broadcast(128), writes=[gqn])
                    S.dma(gkn[:], P["k_g"][j:j + 1, 0:128].partition_broadcast(128), writes=[gkn])
                    S.op(DVE, lambda e: e.tensor_scalar(out=gqn[:], in0=gqn[:], scalar1=qscale, scalar2=None, op0=ALU.mult), [gqn], [gqn])
                    lat = k.sb(ph, [128, 448], F32, "lat")
                    latT = [k.sb(ph, [128, 3, 128], BF16, "latT") for _ in range(2)]
                    qf = k.sb(ph, [128, 1536], F32, "qf")
                    kvf = k.sb(ph, [128, 2048], F32, "kvf")
                    sq = k.sb(ph, [128, 1536], F32, "sq")
                    ssg = k.sb(ph, [128, 32], F32, "ssg")
                    rs = k.sb(ph, [128, 32], F32, "rs")
                    qn = k.sb(ph, [128, 1024], F32, "qn")
                    kn = k.sb(ph, [128, 1024], F32, "kn")
                    qr = k.sb(ph, [128, 512], F32, "qr")
                    qro = k.sb(ph, [128, 512], F32, "qro")
                    krn = k.sb(ph, [128, 64], F32, "krn")
                    kro = k.sb(ph, [128, 64], F32, "kro")
                    qTst = [k.sb(ph, [128, 8, 128], BF16, "qTst") for _ in range(2)]
                    kTst = [k.sb(ph, [128, 8, 128], BF16, "kTst") for _ in range(2)]
                    qRst = [k.sb(ph, [64, 8, 128], BF16, "qRst") for _ in range(2)]
                    kRst = [k.sb(ph, [64, 1, 128], BF16, "kRst") for _ in range(2)]
                    lst = k.sb(ph, [128, 4], F32, "lst")
                pi = 0
                for t in range(NT):
                    b2 = t % 2
                    front.stt_cur = stt_[b2]
                    rstd, nrstd = front(t, xsrc, xt[b2], stt_[b2], xT[b2], psT)
                    sc = stt_[b2]
                    tsl = slice(t * 128, (t + 1) * 128)
                    S.op(DVE, lambda e: e.tensor_tensor(out=TQ[:], in0=cs4[:, t, :], in1=G4q[:], op=ALU.mult), [cs4, G4q], [TQ])
                    S.op(DVE, lambda e: e.tensor_tensor(out=TK[:], in0=cs4[:, t, :], in1=G4k[:], op=ALU.mult), [cs4, G4k], [TK])
                    if kind == "A":
                        for fg in range(8):
                            bank = psP[pi % 4]
                            pi += 1
                            project(xT[b2], W, fg * 512, 512, bank)
                            if fg < 4:
                                S.op(ACT, lambda e: e.activation(out=qf[:, fg * 512:(fg + 1) * 512], in_=bank[:], func=AF.Identity, scale=rstd), [bank, sc], [qf])
                            elif fg < 6:
                                S.op(ACT, lambda e: e.activation(out=Vst[b2][:, (fg - 4) * 4:(fg - 4) * 4 + 4, 0:128], in_=bank[:].rearrange("p (a b) -> p a b", a=4),
                                                                  func=AF.Identity, scale=rstd), [bank, sc], [Vst[b2]])
                            else:
                                h0 = (fg - 6) * 512
                                silu_gate(bank, rstd, nrstd, e_[fg % 2], Gst[b2][:, h0:h0 + 512], Gst[b2])
                        group_rs.srcbuf = qf
                        group_rs(qf[:, 0:2048], 32, 64, sq, ssg, rs)
                        S.op(DVE, lambda e: e.tensor_tensor(out=sq[:].rearrange("p (g d) -> p g d", g=32), in0=qf[:].rearrange("p (g d) -> p g d", g=32),
                                                            in1=rs[:, 0:32].unsqueeze(2).broadcast_to([128, 32, 64]), op=ALU.mult), [qf, rs], [sq])
                        for half, T in ((0, TQ), (1, TK)):
                            for q4 in range(2):
                                o0 = half * 1024 + q4 * 512
                                rope(sq[:, o0:o0 + 512].rearrange("p (g d) -> p g d", g=8), T, ro[:, o0:o0 + 512].rearrange("p (g d) -> p g d", g=8),
                                     ta, tb, 8, sq, T, ro)
                        transp_store(ro, 16, 128, psQ, qTst[b2], 0, ro)
                        S.dma(QT[:, :, tsl].rearrange("h p t -> p h t"), qTst[b2][:, 0:8, :], reads=[qTst[b2]], writes=[QT])
                        S.dma(KT[:, :, tsl].rearrange("h p t -> p h t"), qTst[b2][:, 8:16, :], reads=[qTst[b2]], writes=[KT])
                    else:
                        bank = psP[pi % 4]
                        pi += 1
                        project(xT[b2], W, 0, 448, bank)
                        S.op(ACT, lambda e: e.activation(out=lat[:], in_=bank[:, 0:448], func=AF.Identity, scale=rstd), [bank, sc], [lat])
                        for fg in range(2):
                            bank = psP[pi % 4]
                            pi += 1
                            project(xT[b2], W, 448 + fg * 512, 512, bank)
                            silu_gate(bank, rstd, nrstd, e_[fg % 2], Gst[b2][:, fg * 512:(fg + 1) * 512], Gst[b2])
                        group_rs.srcbuf = lat
                        group_rs(lat[:, 0:256], 1, 256, sq, ssg, lst, off=0)
                        group_rs(lat[:, 256:384], 1, 128, sq, ssg, lst, off=1)
                        group_rs(lat[:, 384:448], 1, 64, sq, ssg, lst, off=2)
                        transp_store(lat, 3, 128, psQ, latT[b2], 0, lat, evac=DVE)
                        for fg in range(3):
                            bank = psP[pi % 4]
                            pi += 1
                            project(latT[b2], Wuq, fg * 512, 512, bank, nch=2)
                            S.op(ACT, lambda e: e.activation(out=qf[:, fg * 512:(fg + 1) * 512], in_=bank[:], func=AF.Identity, scale=lst[:, 0:1]), [bank, lst], [qf])
                        for fg in range(4):
                            bank = psP[pi % 4]
                            pi += 1
                            S.op(PE, lambda e: e.matmul(bank[:], lhsT=latT[b2][:, 2, :], rhs=Wukv[:, 0, fg * 512:(fg + 1) * 512], start=True, stop=True),
                                 [latT[b2], Wukv], [bank])
                            S.op(ACT, lambda e: e.activation(out=kvf[:, fg * 512:(fg + 1) * 512], in_=bank[:], func=AF.Identity, scale=lst[:, 1:2]), [bank, lst], [kvf])
                        q3 = qf[:].rearrange("p (h d) -> p h d", h=8)
                        kv3 = kvf[:].rearrange("p (h d) -> p h d", h=8)
                        sq3 = sq[:].rearrange("p (h d) -> p h d", h=8)
                        S.op(DVE, lambda e: e.tensor_tensor(out=sq[:], in0=qf[:], in1=qf[:], op=ALU.mult), [qf], [sq])
                        S.op(DVE, lambda e: e.tensor_reduce(out=ssg[:, 0:8], in_=sq3[:, :, 0:128], axis=AX.X, op=ALU.add), [sq], [ssg])
                        S.op(DVE, lambda e: e.tensor_reduce(out=ssg[:, 8:16], in_=sq3[:, :, 128:192], axis=AX.X, op=ALU.add), [sq], [ssg])
                        S.op(DVE, lambda e: e.tensor_tensor(out=sq[:, 0:1024].rearrange("p (h d) -> p h d", h=8), in0=kv3[:, :, 0:128], in1=kv3[:, :, 0:128], op=ALU.mult),
                             [kvf], [sq])
                        S.op(DVE, lambda e: e.tensor_reduce(out=ssg[:, 16:24], in_=sq[:, 0:1024].rearrange("p (h d) -> p h d", h=8), axis=AX.X, op=ALU.add), [sq], [ssg])
                        S.op(ACT, lambda e: e.activation(out=ssg[:, 0:8], in_=ssg[:, 0:8], func=AF.Ln, scale=1.0 / 128, bias=EPS), [ssg], [ssg])
                        S.op(ACT, lambda e: e.activation(out=ssg[:, 8:16], in_=ssg[:, 8:16], func=AF.Ln, scale=1.0 / 64, bias=EPS), [ssg], [ssg])
                        S.op(ACT, lambda e: e.activation(out=ssg[:, 16:24], in_=ssg[:, 16:24], func=AF.Ln, scale=1.0 / 128, bias=EPS), [ssg], [ssg])
                        S.op(ACT, lambda e: e.activation(out=rs[:, 0:24], in_=ssg[:, 0:24], func=AF.Exp, scale=-0.5), [ssg], [rs])
                        qn3 = qn[:].rearrange("p (h d) -> p h d", h=8)
                        kn3 = kn[:].rearrange("p (h d) -> p h d", h=8)
                        S.op(DVE, lambda e: e.tensor_tensor(out=qn3, in0=q3[:, :, 0:128], in1=rs[:, 0:8].unsqueeze(2).broadcast_to([128, 8, 128]), op=ALU.mult), [qf, rs], [qn])
                        S.op(DVE, lambda e: e.tensor_tensor(out=qn3, in0=qn3, in1=bc(gqn[:], [128, 8, 128]), op=ALU.mult), [qn, gqn], [qn])
                        S.op(DVE, lambda e: e.tensor_tensor(out=kn3, in0=kv3[:, :, 0:128], in1=rs[:, 16:24].unsqueeze(2).broadcast_to([128, 8, 128]), op=ALU.mult), [kvf, rs], [kn])
                        S.op(DVE, lambda e: e.tensor_tensor(out=kn3, in0=kn3, in1=bc(gkn[:], [128, 8, 128]), op=ALU.mult), [kn, gkn], [kn])
                        qr3 = qr[:].rearrange("p (h d) -> p h d", h=8)
                        S.op(DVE, lambda e: e.tensor_tensor(out=qr3, in0=q3[:, :, 128:192], in1=rs[:, 8:16].unsqueeze(2).broadcast_to([128, 8, 64]), op=ALU.mult), [qf, rs], [qr])
                        rope(qr3, TQ, qro[:].rearrange("p (h d) -> p h d", h=8), ta, tb, 8, qr, TQ, qro)
                        S.op(DVE, lambda e: e.tensor_scalar(out=krn[:], in0=lat[:, 384:448], scalar1=lst[:, 2:3], scalar2=None, op0=ALU.mult), [lat, lst], [krn])
                        rope(krn[:].rearrange("p (g d) -> p g d", g=1), TK, kro[:].rearrange("p (g d) -> p g d", g=1), ta, tb, 1, krn, TK, kro)
                        S.op(ACT, lambda e: e.activation(out=Vst[b2][:, :, 0:128], in_=kv3[:, :, 128:256], func=AF.Copy), [kvf], [Vst[b2]])
                        transp_store(qn, 8, 128, psQ, qTst[b2], 0, qn)
                        transp_store(kn, 8, 128, psQ, kTst[b2], 0, kn)
                        transp_store(qro, 8, 64, psQ, qRst[b2], 0, qro)
                        transp_store(kro, 1, 64, psQ, kRst[b2], 0, kro)
                        S.dma(QT[:, :, tsl].rearrange("h p t -> p h t"), qTst[b2][:], reads=[qTst[b2]], writes=[QT])
                        S.dma(KT[:, :, tsl].rearrange("h p t -> p h t"), kTst[b2][:], reads=[kTst[b2]], writes=[KT])
                        S.dma(QR[:, :, tsl].rearrange("h p t -> p h t"), qRst[b2][:], reads=[qRst[b2]], writes=[QR])
                        S.dma(KR[:, tsl], kRst[b2][:, 0, :], reads=[kRst[b2]], writes=[KR])
                    S.dma(VS[tsl, :, :], Vst[b2][:], reads=[Vst[b2]], writes=[VS])
                    S.dma(GS[tsl, :], Gst[b2][:], reads=[Gst[b2]], writes=[GS])
                S.barrier()

        def phase_T2_attn(kind, j, lambda_init=None):
            with ExitStack() as ph:
                P = A if kind == "A" else C
                nmap = 2 if kind == "A" else 1
                psS = k.ps(ph, [128, 2, 2, 512], name="psS")
                pacc = k.ps(ph, [128, 3, 512], name="pacc")
                psO = k.ps(ph, [128, 512], name="psO")
                psSb = [Buf(psS.t, "psS0"), Buf(psS.t, "psS1")]
                QTh = [k.sb(ph, [128, SEQ], BF16, "QTh") for _ in range(2)]
                KTh = [k.sb(ph, [128, SEQ], BF16, "KTh") for _ in range(2)]
                Vh = [k.sb(ph, [128, NT, 130], BF16, "Vh") for _ in range(2)]
                if kind == "C":
                    QRh = [k.sb(ph, [64, SEQ], BF16, "QRh") for _ in range(2)]
                    KRs = k.sb(ph, [64, SEQ], BF16, "KRs")
                    S.dma(KRs[:], KR[:, :], reads=[KR], writes=[KRs])
                PT = [k.sb(ph, [128, 2, 512], BF16, "PT") for _ in range(3)]
                Gt = [k.sb(ph, [128, 4, 128], BF16, "Gt") for _ in range(2)]
                accsb = k.sb(ph, [128, 3, 512], F32, "accsb")
                rz = k.sb(ph, [128, 8], F32, "rz")
                tmp = k.sb(ph, [128, 128], F32, "tmp")
                o4 = k.sb(ph, [128, 4, 128], F32, "o4")
                ogf = k.sb(ph, [128, 4, 128], F32, "ogf")
                junk = k.sb(ph, [128, 128], F32, "junk")
                ssj = k.sb(ph, [128, 8], F32, "ssj")
                ogst = [k.sb(ph, [128, 512], BF16, "ogst") for _ in range(2)]
                neglam = None
                if kind == "A":
                    lv = k.sb(ph, [128, 4, 64], F32, "lv")
                    for i, nm in enumerate(("lq1", "lk1", "lq2", "lk2")):
                        S.dma(lv[:, i, :], P[nm][j:j + 1, :].partition_broadcast(128), writes=[lv])
                    lam = k.sb(ph, [128, 4], F32, "lam")
                    S.op(DVE, lambda e: e.scalar_tensor_tensor(out=junk[:, 0:64], in0=lv[:, 0, :], scalar=1.0, in1=lv[:, 1, :], op0=ALU.mult, op1=ALU.mult,
                                                               accum_out=lam[:, 0:1]), [lv], [junk, lam])
                    S.op(DVE, lambda e: e.scalar_tensor_tensor(out=junk[:, 64:128], in0=lv[:, 2, :], scalar=1.0, in1=lv[:, 3, :], op0=ALU.mult, op1=ALU.mult,
                                                               accum_out=lam[:, 1:2]), [lv], [junk, lam])
                    S.op(ACT, lambda e: e.activation(out=lam[:, 0:2], in_=lam[:, 0:2], func=AF.Exp), [lam], [lam])
                    S.op(DVE, lambda e: e.scalar_tensor_tensor(out=lam[:, 2:3], in0=lam[:, 1:2], scalar=-float(lambda_init), in1=lam[:, 0:1], op0=ALU.add, op1=ALU.subtract),
                         [lam], [lam])
                    neglam = lam
                offs = lambda i: (i // 3) * 512 + (i % 3) * 130
                accf = pacc.t[:].rearrange("p a b -> p (a b)")
                accsf = accsb.t[:].rearrange("p a b -> p (a b)")

                def load_head(h):
                    b = h % 2
                    S.dma(QTh[b][:], QT[h, :, :], reads=[QT], writes=[QTh[b]])
                    S.dma(KTh[b][:], KT[h, :, :], reads=[KT], writes=[KTh[b]])
                    S.dma(Vh[b][:], VS[:, h, :].rearrange("(t p) d -> p t d", p=128), reads=[VS], writes=[Vh[b]])
                    if kind == "C":
                        S.dma(QRh[b][:], QR[h, :, :], reads=[QR], writes=[QRh[b]])

                jobs = []
                for h in range(8):
                    for qb in range(8):
                        nk = 4 * qb + 4
                        for kt in range(nk):
                            jobs.append((h, qb, kt, kt == 0, kt == nk - 1))

                def stage_S(i, job):
                    h, qb, kt, first, last = job
                    hb = h % 2
                    jj = kt - 4 * qb
                    q0 = qb * 512 + (max(jj, 0)) * 128
                    n = (qb + 1) * 512 - q0
                    sb_ = psSb[i % 2]
                    pt = PT[i % 3]
                    ks = slice(kt * 128, (kt + 1) * 128)
                    if kind == "A":
                        for m in range(2):
                            rows = slice(m * 64, (m + 1) * 64)
                            S.op(PE, lambda e: e.matmul(psS.t[:, i % 2, m, 0:n], lhsT=KTh[hb][rows, ks], rhs=QTh[hb][rows, q0:q0 + n], start=True, stop=True),
                                 [KTh[hb], QTh[hb]], [sb_])
                        S.op(ACT, lambda e: e.activation(out=pt[:, :, 0:n], in_=psS.t[:, i % 2, :, 0:n], func=AF.Exp), [sb_], [pt])
                        if jj >= 0:
                            S.op(DVE, lambda e: e.memset(pt[64:128, :, 0:64], 0.0), [], [pt])
                    else:
                        S.op(PE, lambda e: e.matmul(psS.t[:, i % 2, 0, 0:n], lhsT=KTh[hb][:, ks], rhs=QTh[hb][:, q0:q0 + n], start=True, stop=False),
                             [KTh[hb], QTh[hb]], [sb_])
                        S.op(PE, lambda e: e.matmul(psS.t[:, i % 2, 0, 0:n], lhsT=KRs[:, ks], rhs=QRh[hb][:, q0:q0 + n], start=False, stop=True),
                             [KRs, QRh[hb]], [sb_])
                        S.op(ACT, lambda e: e.activation(out=pt[:, 0, 0:n], in_=psS.t[:, i % 2, 0, 0:n], func=AF.Exp), [sb_], [pt])
                        if jj >= 0:
                            S.op(DVE, lambda e: e.memset(pt[64:128, 0, 0:64], 0.0), [], [pt])

                def stage_PV(i, job):
                    h, qb, kt, first, last = job
                    hb = h % 2
                    jj = max(kt - 4 * qb, 0)
                    pt = PT[i % 3]
                    for m in range(nmap):
                        for jq in range(jj, 4):
                            ai = m * 4 + jq
                            o = offs(ai)
                            st_ = first and (ai % 3 == 0)
                            S.op(PE, lambda e: e.matmul(accf[:, o:o + 130], lhsT=pt[:, m, (jq - jj) * 128:(jq - jj + 1) * 128], rhs=Vh[hb][:, kt, :],
                                                        start=st_, stop=last, skip_group_check=True), [pt, Vh[hb]], [pacc])
                    if last:
                        post(h, qb)

                def post(h, qb):
                    gb = Gt[(h * 8 + qb) % 2]
                    ob = ogst[(h * 8 + qb) % 2]
                    S.dma(gb[:], GS[qb * 512:(qb + 1) * 512, h * 128:(h + 1) * 128].rearrange("(j p) d -> p j d", p=128), reads=[GS], writes=[gb])
                    nb = 3 if kind == "A" else 2
                    for b in range(nb):
                        S.op(DVE, lambda e: e.tensor_copy(out=accsb[:, b, :], in_=pacc[:, b, :]), [pacc], [accsb])
                    for jq in range(4):
                        o0 = offs(jq)
                        S.op(DVE, lambda e: e.reciprocal(out=rz[:, 0:1], in_=accsf[:, o0 + 128:o0 + 129]), [accsb], [rz])
                        if kind == "A":
                            o1 = offs(4 + jq)
                            S.op(DVE, lambda e: e.reciprocal(out=rz[:, 1:2], in_=accsf[:, o1 + 128:o1 + 129]), [accsb], [rz])
                            S.op(DVE, lambda e: e.tensor_tensor(out=rz[:, 2:3], in0=rz[:, 1:2], in1=neglam[:, 2:3], op=ALU.mult), [rz, neglam], [rz])
                            S.op(DVE, lambda e: e.tensor_scalar(out=tmp[:], in0=accsf[:, o1:o1 + 128], scalar1=rz[:, 2:3], scalar2=None, op0=ALU.mult), [accsb, rz], [tmp])
                            S.op(DVE, lambda e: e.scalar_tensor_tensor(out=o4[:, jq, :], in0=accsf[:, o0:o0 + 128], scalar=rz[:, 0:1], in1=tmp[:], op0=ALU.mult, op1=ALU.add),
                                 [accsb, rz, tmp], [o4])
                            S.op(DVE, lambda e: e.scalar_tensor_tensor(out=junk[:], in0=o4[:, jq, :], scalar=1.0, in1=o4[:, jq, :], op0=ALU.mult, op1=ALU.mult,
                                                                       accum_out=ssj[:, jq:jq + 1]), [o4], [junk, ssj])
                        else:
                            S.op(DVE, lambda e: e.scalar_tensor_tensor(out=ogf[:, jq, :], in0=accsf[:, o0:o0 + 128], scalar=rz[:, 0:1], in1=gb[:, jq, :], op0=ALU.mult, op1=ALU.mult),
                                 [accsb, rz, gb], [ogf])
                    if kind == "A":
                        S.op(ACT, lambda e: e.activation(out=ssj[:, 4:8], in_=ssj[:, 0:4], func=AF.Ln, scale=1.0 / 128, bias=EPS), [ssj], [ssj])
                        S.op(ACT, lambda e: e.activation(out=ssj[:, 4:8], in_=ssj[:, 4:8], func=AF.Exp, scale=-0.5), [ssj], [ssj])
                        for jq in range(4):
                            S.op(DVE, lambda e: e.scalar_tensor_tensor(out=ogf[:, jq, :], in0=o4[:, jq, :], scalar=ssj[:, 4 + jq:5 + jq], in1=gb[:, jq, :], op0=ALU.mult, op1=ALU.mult),
                                 [o4, ssj, gb], [ogf])
                    for jq in range(4):
                        S.op(PE, lambda e: e.transpose(out=psO[:, jq * 128:(jq + 1) * 128], in_=ogf[:, jq, :], identity=ident[:]), [ogf, ident], [psO])
                    S.op(DVE, lambda e: e.tensor_copy(out=ob[:], in_=psO[:]), [psO], [ob])
                    S.dma(OGT[h, :, qb * 512:(qb + 1) * 512], ob[:], reads=[ob], writes=[OGT])

                load_head(0)
                pend = None
                for i, job in enumerate(jobs):
                    stage_S(i, job)
                    if pend is not None:
                        stage_PV(*pend)
                    pend = (i, job)
                    if job[1] == 0 and job[2] == 0 and job[0] + 1 < 8:
                        load_head(job[0] + 1)
                stage_PV(*pend)
                S.barrier()

        def phase_T3(w_out_ap, xsrc, xdst, rowscale=None):
            with ExitStack() as ph:
                Wo = k.sb(ph, [128, 8, DM], BF16, "Wo")
                psP = [k.ps(ph, [128, 512]) for _ in range(4)]
                if rowscale is not None:
                    stage = [k.sb(ph, [128, 2048], F32, "stg") for _ in range(3)]
                    load_w(ph, Wo, w_out_ap, 8, DM, gcol=rowscale[0], stage=stage, gfn=lambda c: rowscale[0][:, 0:1])
                else:
                    load_w(ph, Wo, w_out_ap, 8, DM)
                ogT = [k.sb(ph, [128, 8, 512], BF16, "ogT") for _ in range(2)]
                xb = [k.sb(ph, [128, 4, DM], F32, "xb") for _ in range(2)]
                pi = 0
                for tb_ in range(8):
                    b = tb_ % 2
                    tsl = slice(tb_ * 512, (tb_ + 1) * 512)
                    S.dma(ogT[b][:], OGT[:, :, tsl].rearrange("h p t -> p h t"), reads=[OGT], writes=[ogT[b]])
                    S.dma(xb[b][:], xsrc[tsl, :].rearrange("(j p) d -> p j d", p=128), reads=[xsrc], writes=[xb[b]])
                    for jq in range(4):
                        for ng in range(2):
                            bank = psP[pi % 4]
                            pi += 1
                            for c in range(8):
                                S.op(PE, lambda e: e.matmul(bank[:], lhsT=ogT[b][:, c, jq * 128:(jq + 1) * 128], rhs=Wo[:, c, ng * 512:(ng + 1) * 512],
                                                            start=(c == 0), stop=(c == 7)), [ogT[b], Wo], [bank])
                            S.op(DVE, lambda e: e.tensor_tensor(out=xb[b][:, jq, ng * 512:(ng + 1) * 512], in0=bank[:], in1=xb[b][:, jq, ng * 512:(ng + 1) * 512], op=ALU.add),
                                 [bank, xb[b]], [xb[b]])
                    S.dma(xdst[tsl, :].rearrange("(j p) d -> p j d", p=128), xb[b][:], reads=[xb[b]], writes=[xdst])
                S.barrier()

        def subg_scale(ph_stack, j, lambda_init):
            sg = k.sb(ph_stack, [128, 1], F32, "subg")
            pst = k.ps(ph_stack, [128, 512])
            cv = colvec(ph_stack, pst, A["sub_g"][j:j + 1, :], 1)
            S.op(DVE, lambda e: e.tensor_scalar(out=sg[:], in0=cv[:], scalar1=float(1.0 - lambda_init), scalar2=None, op0=ALU.mult), [cv], [sg])
            return sg

        def phase_T1_B(xsrc):
            with ExitStack() as ph:
                W = k.sb(ph, [128, 8, 3072], BF16, "W")
                stage = [k.sb(ph, [128, 2048], F32, "stg") for _ in range(3)]
                psT = [k.ps(ph, [128, 512]) for _ in range(2)]
                psP = [k.ps(ph, [128, 512]) for _ in range(4)]
                psQ = [k.ps(ph, [128, 512]) for _ in range(2)]
                gcol = colvec(ph, psT[0], Bp["norm_g"][0:1, :].rearrange("o (c p) -> (o c) p", p=128), 8)
                load_w(ph, W, Bp["w_in"][0], 8, 3072, gcol=gcol, stage=stage, gfn=lambda c: gcol[:, c:c + 1])
                brow = k.sb(ph, [128, 3072], F32, "brow")
                S.dma(brow[:], Bp["b_in"][0:1, :].partition_broadcast(128), writes=[brow])
                front.brow = brow
                xt = [k.sb(ph, [128, DM], F32, "xt") for _ in range(2)]
                stt_ = [k.sb(ph, [128, 4], F32, "stt") for _ in range(2)]
                xT = [k.sb(ph, [128, 8, 128], BF16, "xT") for _ in range(2)]
                front.junk = k.sb(ph, [128, DM], F32, "junk")
                af = k.sb(ph, [128, DM], F32, "af")
                e_ = [k.sb(ph, [128, 512], F32, "e") for _ in range(2)]
                gfb = [k.sb(ph, [128, 512], F32, "gf") for _ in range(2)]
                u = k.sb(ph, [128, DM], F32, "u")
                gg = k.sb(ph, [128, DM], F32, "gg")
                uTst = [k.sb(ph, [128, 8, 128], BF16, "uTst") for _ in range(2)]
                gTst = [k.sb(ph, [128, 8, 128], BF16, "gTst") for _ in range(2)]
                pi = 0
                for t in range(NT):
                    b2 = t % 2
                    front.stt_cur = stt_[b2]
                    rstd, nrstd = front(t, xsrc, xt[b2], stt_[b2], xT[b2], psT)
                    sc = stt_[b2]
                    tsl = slice(t * 128, (t + 1) * 128)
                    for fg in range(6):
                        bank = psP[pi % 4]
                        pi += 1
                        project(xT[b2], W, fg * 512, 512, bank)
                        h0 = (fg % 2) * 512
                        if fg < 2:
                            S.op(DVE, lambda e: e.scalar_tensor_tensor(out=af[:, h0:h0 + 512], in0=bank[:], scalar=rstd, in1=brow[:, fg * 512:(fg + 1) * 512],
                                                                       op0=ALU.mult, op1=ALU.add), [bank, sc, brow], [af])
                        elif fg < 4:
                            gf = gfb[fg % 2]
                            ee = e_[fg % 2]
                            S.op(DVE, lambda e: e.scalar_tensor_tensor(out=gf[:], in0=bank[:], scalar=rstd, in1=brow[:, fg * 512:(fg + 1) * 512],
                                                                       op0=ALU.mult, op1=ALU.add), [bank, sc, brow], [gf])
                            S.op(ACT, lambda e: e.activation(out=ee[:], in_=gf[:], func=AF.Exp, scale=-1.0), [gf], [ee])
                            S.op(DVE, lambda e: e.tensor_scalar(out=ee[:], in0=ee[:], scalar1=1.0, scalar2=None, op0=ALU.add), [ee], [ee])
                            S.op(DVE, lambda e: e.reciprocal(out=ee[:], in_=ee[:]), [ee], [ee])
                            S.op(DVE, lambda e: e.tensor_tensor(out=u[:, h0:h0 + 512], in0=af[:, h0:h0 + 512], in1=ee[:], op=ALU.mult), [af, ee], [u])
                        else:
                            silu_gate(bank, rstd, nrstd, e_[fg % 2], gg[:, h0:h0 + 512], gg, bias_row=brow[:, fg * 512:(fg + 1) * 512], gf=gfb[fg % 2])
                    transp_store(u, 8, 128, psQ, uTst[b2], 0, u)
                    transp_store(gg, 8, 128, psQ, gTst[b2], 0, gg)
                    S.dma(QT[:, :, tsl].rearrange("h p t -> p h t"), uTst[b2][:], reads=[uTst[b2]], writes=[QT])
                    S.dma(KT[:, :, tsl].rearrange("h p t -> p h t"), gTst[b2][:], reads=[gTst[b2]], writes=[KT])
                S.barrier()

        def phase_T2_B(xsrc, xdst):
            with ExitStack() as ph:
                psC = [k.ps(ph, [128, 512]) for _ in range(2)]
                psS1 = k.ps(ph, [128, 512])
                psS2 = k.ps(ph, [128, 512])
                psP = [k.ps(ph, [128, 512]) for _ in range(3)]
                psX = k.ps(ph, [128, 512])
                Wo = k.sb(ph, [128, 8, DM], BF16, "Wo")
                load_w(ph, Wo, Bp["w_out"][0], 8, DM)
                cw = k.sb(ph, [31, DM], F32, "cw")
                S.dma(cw[:], Bp["conv_w"][0], writes=[cw])
                wcol = k.sb(ph, [128, 8, 31], F32, "wcol")
                for ct in range(8):
                    S.op(PE, lambda e: e.transpose(out=psX[:, ct * 32:ct * 32 + 31], in_=cw[:, ct * 128:(ct + 1) * 128], identity=ident[0:31, 0:31]), [cw, ident], [psX])
                S.op(DVE, lambda e: e.tensor_copy(out=wcol[:], in_=psX[:, 0:256].rearrange("p (a b) -> p a b", a=8)[:, :, 0:31]), [psX], [wcol])
                cb = colvec(ph, psX, Bp["conv_b"][0:1, :].rearrange("o (c p) -> (o c) p", p=128), 8)
                lng = colvec(ph, psX, Bp["ln_g"][0:1, :].rearrange("o (c p) -> (o c) p", p=128), 8)
                lnb = colvec(ph, psX, Bp["ln_b"][0:1, :].rearrange("o (c p) -> (o c) p", p=128), 8)
                diag = k.sb(ph, [128, 31, 8, 128], BF16, "diag")
                for kk in range(31):
                    for ct in range(8):
                        eng = DVE if (kk * 8 + ct) % 3 else POOL
                        S.op(eng, lambda e: e.tensor_scalar(out=diag[:, kk, ct, :], in0=ident[:], scalar1=wcol[:, ct, kk:kk + 1], scalar2=None, op0=ALU.mult),
                             [ident, wcol], [diag])
                ones = k.sb(ph, [128, 128], F32, "ones")
                S.op(DVE, lambda e: e.memset(ones[:], 1.0), [], [ones])
                UTb = k.sb(ph, [128, 8, 542], BF16, "UTb")
                GTb = k.sb(ph, [128, 8, 512], BF16, "GTb")
                xb = k.sb(ph, [128, 4, DM], F32, "xb")
                v = k.sb(ph, [128, 8, 512], F32, "v")
                sqt = [k.sb(ph, [128, 512], F32, "sqt") for _ in range(2)]
                mean = k.sb(ph, [128, 512], F32, "mean")
                m2 = k.sb(ph, [128, 512], F32, "m2")
                rstdb = k.sb(ph, [128, 512], F32, "rstdb")
                tt = [k.sb(ph, [128, 512], F32, "tt") for _ in range(2)]
                ee = [k.sb(ph, [128, 512], F32, "ee") for _ in range(2)]
                zg = k.sb(ph, [128, 8, 512], BF16, "zg")
                ci = 0
                pi = 0
                for tb_ in range(8):
                    tsl = slice(tb_ * 512, (tb_ + 1) * 512)
                    if tb_ == 0:
                        S.op(DVE, lambda e: e.memset(UTb[:, :, 0:30], 0.0), [], [UTb])
                        S.dma(UTb[:, :, 30:542], QT[:, :, 0:512].rearrange("h p t -> p h t"), reads=[QT], writes=[UTb])
                    else:
                        S.dma(UTb[:], QT[:, :, tb_ * 512 - 30:(tb_ + 1) * 512].rearrange("h p t -> p h t"), reads=[QT], writes=[UTb])
                    S.dma(GTb[:], KT[:, :, tsl].rearrange("h p t -> p h t"), reads=[KT], writes=[GTb])
                    S.dma(xb[:], xsrc[tsl, :].rearrange("(j p) d -> p j d", p=128), reads=[xsrc], writes=[xb])
                    for ct in range(8):
                        bank = psC[ci % 2]
                        ci += 1
                        for kk in range(31):
                            S.op(PE, lambda e: e.matmul(bank[:], lhsT=diag[:, kk, ct, :], rhs=UTb[:, ct, kk:kk + 512], start=(kk == 0), stop=(kk == 30)),
                                 [diag, UTb], [bank])
                        S.op(ACT, lambda e: e.activation(out=v[:, ct, :], in_=bank[:], func=AF.Identity, bias=cb[:, ct:ct + 1]), [bank, cb], [v])
                        sq_ = sqt[ct % 2]
                        S.op(ACT, lambda e: e.activation(out=sq_[:], in_=v[:, ct, :], func=AF.Square), [v], [sq_])
                        S.op(PE, lambda e: e.matmul(psS1[:], lhsT=ones[:], rhs=v[:, ct, :], start=(ct == 0), stop=(ct == 7)), [ones, v], [psS1])
                        S.op(PE, lambda e: e.matmul(psS2[:], lhsT=ones[:], rhs=sq_[:], start=(ct == 0), stop=(ct == 7)), [ones, sq_], [psS2])
                    S.op(ACT, lambda e: e.activation(out=mean[:], in_=psS1[:], func=AF.Identity, scale=1.0 / DM), [psS1], [mean])
                    S.op(DVE, lambda e: e.tensor_tensor(out=m2[:], in0=mean[:], in1=mean[:], op=ALU.mult), [mean], [m2])
                    S.op(DVE, lambda e: e.scalar_tensor_tensor(out=m2[:], in0=psS2[:], scalar=1.0 / DM, in1=m2[:], op0=ALU.mult, op1=ALU.subtract), [psS2, m2], [m2])
                    S.op(ACT, lambda e: e.activation(out=m2[:], in_=m2[:], func=AF.Ln, bias=EPS), [m2], [m2])
                    S.op(ACT, lambda e: e.activation(out=rstdb[:], in_=m2[:], func=AF.Exp, scale=-0.5), [m2], [rstdb])
                    for ct in range(8):
                        t_ = tt[ct % 2]
                        e2 = ee[ct % 2]
                        S.op(DVE, lambda e: e.tensor_tensor(out=t_[:], in0=v[:, ct, :], in1=mean[:], op=ALU.subtract), [v, mean], [t_])
                        S.op(DVE, lambda e: e.tensor_tensor(out=t_[:], in0=t_[:], in1=rstdb[:], op=ALU.mult), [t_, rstdb], [t_])
                        S.op(DVE, lambda e: e.tensor_scalar(out=t_[:], in0=t_[:], scalar1=lng[:, ct:ct + 1], scalar2=lnb[:, ct:ct + 1], op0=ALU.mult, op1=ALU.add),
                             [t_, lng, lnb], [t_])
                        S.op(ACT, lambda e: e.activation(out=e2[:], in_=t_[:], func=AF.Exp, scale=-1.0), [t_], [e2])
                        S.op(DVE, lambda e: e.tensor_scalar(out=e2[:], in0=e2[:], scalar1=1.0, scalar2=None, op0=ALU.add), [e2], [e2])
                        S.op(DVE, lambda e: e.reciprocal(out=e2[:], in_=e2[:]), [e2], [e2])
                        S.op(DVE, lambda e: e.tensor_tensor(out=t_[:], in0=t_[:], in1=e2[:], op=ALU.mult), [t_, e2], [t_])
                        S.op(DVE, lambda e: e.tensor_tensor(out=zg[:, ct, :], in0=t_[:], in1=GTb[:, ct, :], op=ALU.mult), [t_, GTb], [zg])
                    for jq in range(4):
                        for ng in range(2):
                            bank = psP[pi % 3]
                            pi += 1
                            for c in range(8):
                                S.op(PE, lambda e: e.matmul(bank[:], lhsT=zg[:, c, jq * 128:(jq + 1) * 128], rhs=Wo[:, c, ng * 512:(ng + 1) * 512],
                                                            start=(c == 0), stop=(c == 7)), [zg, Wo], [bank])
                            S.op(DVE, lambda e: e.tensor_tensor(out=xb[:, jq, ng * 512:(ng + 1) * 512], in0=bank[:], in1=xb[:, jq, ng * 512:(ng + 1) * 512], op=ALU.add),
                                 [bank, xb], [xb])
                    S.dma(xdst[tsl, :].rearrange("(j p) d -> p j d", p=128), xb[:], reads=[xb], writes=[xdst])
                S.barrier()

        S.barrier()
        cur = XIN
        for li in range(nlayers):
            kind, j = li % 3, li // 3
            if kinds is not None:
                kind, j = kinds[li], 0
            dst = OUT if li == nlayers - 1 else XS[li % 2]
            if kind == 0:
                lam0 = 0.8 - 0.6 * math.exp(-0.3 * li)
                phase_T1_attn("A", j, cur)
                phase_T2_attn("A", j, lambda_init=lam0)
                with ExitStack() as ph0:
                    sg = subg_scale(ph0, j, lam0)
                    phase_T3(A["w_out"][j], cur, dst, rowscale=(sg,))
            elif kind == 1:
                phase_T1_B(cur)
                phase_T2_B(cur, dst)
            else:
                phase_T1_attn("C", j, cur)
                phase_T2_attn("C", j)
                phase_T3(C["w_out"][j], cur, dst)
            cur = dst
        if dbg_out:
            for nm, ap_ in dbg_out.items():
                src = {"qt": QT, "kt": KT, "vs": VS, "gs": GS, "ogt": OGT, "qr": QR, "kr": KR}[nm]
                S.dma(ap_, src.t, reads=[src], writes=[OUT])
        for e2, c in sorted(OUT.writers.items()):
            nc.sync.wait_ge(S.sems[e2], S._semval(e2, c))
        build_program.stats = (S.ninst, S.nwaits)
    return nc


def _consts():
    inv = 1.0 / (10000.0 ** (np.arange(0, 64, 2, dtype=np.float32) / np.float32(64)))
    ang = np.arange(SEQ, dtype=np.float32)[:, None] * inv.astype(np.float32)[None, :]
    ang = ang.astype(np.float32)
    c, s = np.cos(ang).astype(np.float32), np.sin(ang).astype(np.float32)
    cs4 = np.concatenate([c, s, c, s], axis=1).astype(np.float32)
    return np.eye(128, dtype=np.float32), np.ascontiguousarray(cs4)


_NC_CACHE = {}


def kernel(**inputs):
    x = np.ascontiguousarray(np.asarray(inputs["x"], dtype=np.float32))
    if "nc" not in _NC_CACHE:
        _NC_CACHE["nc"] = build_program()
    nc = _NC_CACHE["nc"]
    ident, cs4 = _consts()
    shared = {kname: np.ascontiguousarray(np.asarray(v, dtype=np.float32)) for kname, v in inputs.items() if kname != "x"}
    shared["ident"] = ident
    shared["cs4"] = cs4
    in_maps = []
    for c in range(NCORES):
        m = dict(shared)
        m["x"] = x[c]
        in_maps.append(m)
    res = run_bass_kernel_spmd(nc, in_maps, core_ids=list(range(NCORES)))
    return np.stack([np.asarray(r["out"], dtype=np.float32) for r in res.results], axis=0)
```
